# Optimizing a Trainium2 kernel written in Bass

```python
import math
import jax, jax.numpy as jnp
from jax import lax
import numpy as np

D_MODEL = 1024
BATCH = 16
SEQ = 4096
DEPTH = 1

MLA_HEADS = 4
MLA_NOPE = 128
MLA_ROPE = 64
MLA_V = 128
Q_LORA = D_MODEL // 4
KV_LORA = D_MODEL // 8
MLA_QK = MLA_NOPE + MLA_ROPE
MLA_WIDTH = MLA_HEADS * MLA_V

DIFF_HEADS = 4
DIFF_D = 64
DIFF_V = 2 * DIFF_D
DIFF_QK_WIDTH = DIFF_HEADS * 2 * DIFF_D
DIFF_WIDTH = DIFF_HEADS * DIFF_V

MIX_WIDTH = MLA_WIDTH + DIFF_WIDTH
IN_WIDTH = Q_LORA + KV_LORA + MLA_ROPE + 2 * DIFF_QK_WIDTH + DIFF_WIDTH
SPLITS = (Q_LORA, Q_LORA + KV_LORA, Q_LORA + KV_LORA + MLA_ROPE,
          Q_LORA + KV_LORA + MLA_ROPE + DIFF_QK_WIDTH,
          Q_LORA + KV_LORA + MLA_ROPE + 2 * DIFF_QK_WIDTH)

D_FF = 4 * D_MODEL
N_BUCKETS = 32
MAX_DISTANCE = 128
ROPE_THETA = 10000.0
EPS = 1e-6
Q_BLOCK = 128

kernel_name = "hybrid_mla_diffattn_encoder_layer"


def _rmsnorm(x, w):
    x32 = x.astype(jnp.float32)
    y = x32 * lax.rsqrt(jnp.mean(x32 * x32, axis=-1, keepdims=True) + EPS)
    return (y * w.astype(jnp.float32)).astype(x.dtype)


def _rotate_half(x):
    x1, x2 = jnp.split(x, 2, axis=-1)
    return jnp.concatenate([-x2, x1], axis=-1)


def _rope_tables(seq, dtype):
    inv = ROPE_THETA ** (-jnp.arange(0, MLA_ROPE, 2, dtype=jnp.float32) / MLA_ROPE)
    ang = jnp.arange(seq, dtype=jnp.float32)[:, None] * inv[None, :]
    ang = jnp.concatenate([ang, ang], axis=-1)
    return jnp.cos(ang).astype(dtype), jnp.sin(ang).astype(dtype)


def _t5_bucket(rel):
    half = N_BUCKETS // 2
    ret = jnp.where(rel > 0, half, 0)
    n = jnp.abs(rel)
    max_exact = half // 2
    large = max_exact + (jnp.log(jnp.maximum(n, 1).astype(jnp.float32) / max_exact)
                         / math.log(MAX_DISTANCE / max_exact)
                         * (half - max_exact)).astype(jnp.int32)
    large = jnp.minimum(large, half - 1)
    return ret + jnp.where(n < max_exact, n, large)


def _query_blocks(q):
    b, s = q.shape[:2]
    qb = q.reshape((b, s // Q_BLOCK, Q_BLOCK) + q.shape[2:])
    return jnp.moveaxis(qb, 1, 0)


def _merge_blocks(o):
    o = jnp.moveaxis(o, 0, 1)
    return o.reshape((o.shape[0], o.shape[1] * o.shape[2]) + o.shape[3:])


def _mla_attention(q, k, v):
    scale = MLA_QK ** -0.5

    def one(qblk):
        s = jnp.einsum('bqhd,bkhd->bhqk', qblk, k).astype(jnp.float32) * scale
        p = jax.nn.softmax(s, axis=-1).astype(v.dtype)
        return jnp.einsum('bhqk,bkhd->bqhd', p, v)

    return _merge_blocks(lax.map(one, _query_blocks(q)))


def _diff_attention(q, k, v, rel_bias, lam):
    seq = q.shape[1]
    scale = DIFF_D ** -0.5
    kpos = jnp.arange(seq, dtype=jnp.int32)
    starts = jnp.arange(seq // Q_BLOCK, dtype=jnp.int32) * Q_BLOCK
    table = rel_bias.astype(jnp.float32)

    def one(args):
        qblk, start = args
        qpos = start + jnp.arange(Q_BLOCK, dtype=jnp.int32)
        bucket = _t5_bucket(kpos[None, :] - qpos[:, None])
        bias = jnp.moveaxis(table[bucket], -1, 0)
        s = jnp.einsum('bqhmd,bkhmd->bmhqk', qblk, k).astype(jnp.float32) * scale + bias
        p = jax.nn.softmax(s, axis=-1)
        a = p[:, 0] - lam * p[:, 1]
        return jnp.einsum('bhqk,bkhd->bqhd', a.astype(v.dtype), v)

    return _merge_blocks(lax.map(one, (_query_blocks(q), starts)))


def setup_inputs(seed: int = 0) -> dict:
    key = jax.random.key(seed)
    ks = jax.random.split(key, 24)
    f32 = jnp.float32

    def w(k, shape, fan_in):
        return jax.random.normal(k, shape, f32) * fan_in ** -0.5

    def gain(k, shape):
        return 1.0 + 0.02 * jax.random.normal(k, shape, f32)

    L = DEPTH
    return {
        "x": jax.random.normal(ks[0], (BATCH, SEQ, D_MODEL), f32),
        "attn_norm_w": gain(ks[1], (L, D_MODEL)),
        "w_in": w(ks[2], (L, D_MODEL, IN_WIDTH), D_MODEL),
        "q_a_norm_w": gain(ks[3], (L, Q_LORA)),
        "w_uq": w(ks[4], (L, Q_LORA, MLA_HEADS * MLA_QK), Q_LORA),
        "kv_a_norm_w": gain(ks[5], (L, KV_LORA)),
        "w_ukv": w(ks[6], (L, KV_LORA, MLA_HEADS * (MLA_NOPE + MLA_V)), KV_LORA),
        "mla_q_norm_w": gain(ks[7], (L, MLA_QK)),
        "mla_k_norm_w": gain(ks[8], (L, MLA_QK)),
        "diff_q_norm_w": gain(ks[9], (L, DIFF_D)),
        "diff_k_norm_w": gain(ks[10], (L, DIFF_D)),
        "lambda_q1": 0.1 * jax.random.normal(ks[11], (L, DIFF_D), f32),
        "lambda_k1": 0.1 * jax.random.normal(ks[12], (L, DIFF_D), f32),
        "lambda_q2": 0.1 * jax.random.normal(ks[13], (L, DIFF_D), f32),
        "lambda_k2": 0.1 * jax.random.normal(ks[14], (L, DIFF_D), f32),
        "diff_out_norm_w": gain(ks[15], (L, DIFF_V)),
        "w_out": w(ks[16], (L, MIX_WIDTH, D_MODEL), MIX_WIDTH),
        "mlp_norm_w": gain(ks[17], (L, D_MODEL)),
        "w_up": w(ks[18], (L, D_MODEL, D_FF), D_MODEL),
        "w_down": w(ks[19], (L, D_FF, D_MODEL), D_FF),
        "rel_bias": 0.5 * jax.random.normal(ks[20], (N_BUCKETS, DIFF_HEADS), f32),
    }


def reference(x, attn_norm_w, w_in, q_a_norm_w, w_uq, kv_a_norm_w, w_ukv,
              mla_q_norm_w, mla_k_norm_w, diff_q_norm_w, diff_k_norm_w,
              lambda_q1, lambda_k1, lambda_q2, lambda_k2, diff_out_norm_w,
              w_out, mlp_norm_w, w_up, w_down, rel_bias):
    b, s, _ = x.shape
    cos, sin = _rope_tables(s, x.dtype)
    for layer in range(DEPTH):
        lam_init = 0.8 - 0.6 * math.exp(-0.3 * layer)

        h = _rmsnorm(x, attn_norm_w[layer])
        proj = h @ w_in[layer]
        c_q, c_kv, k_rope, dq, dk, dv = jnp.split(proj, SPLITS, axis=-1)

        q = (_rmsnorm(c_q, q_a_norm_w[layer]) @ w_uq[layer]).reshape(b, s, MLA_HEADS, MLA_QK)
        q_nope, q_rope = q[..., :MLA_NOPE], q[..., MLA_NOPE:]
        q_rope = q_rope * cos[:, None, :] + _rotate_half(q_rope) * sin[:, None, :]
        kv = (_rmsnorm(c_kv, kv_a_norm_w[layer]) @ w_ukv[layer]).reshape(b, s, MLA_HEADS, MLA_NOPE + MLA_V)
        k_nope, v_mla = kv[..., :MLA_NOPE], kv[..., MLA_NOPE:]
        k_rope = k_rope * cos + _rotate_half(k_rope) * sin
        k_rope = jnp.broadcast_to(k_rope[:, :, None, :], (b, s, MLA_HEADS, MLA_ROPE))
        q_m = _rmsnorm(jnp.concatenate([q_nope, q_rope], axis=-1), mla_q_norm_w[layer])
        k_m = _rmsnorm(jnp.concatenate([k_nope, k_rope], axis=-1), mla_k_norm_w[layer])
        o_mla = _mla_attention(q_m, k_m, v_mla).reshape(b, s, MLA_WIDTH)

        dq = _rmsnorm(dq.reshape(b, s, DIFF_HEADS, 2, DIFF_D), diff_q_norm_w[layer])
        dk = _rmsnorm(dk.reshape(b, s, DIFF_HEADS, 2, DIFF_D), diff_k_norm_w[layer])
        dv = dv.reshape(b, s, DIFF_HEADS, DIFF_V)
        lam = (jnp.exp(jnp.sum(lambda_q1[layer].astype(jnp.float32) * lambda_k1[layer].astype(jnp.float32)))
               - jnp.exp(jnp.sum(lambda_q2[layer].astype(jnp.float32) * lambda_k2[layer].astype(jnp.float32)))
               + lam_init)
        o_d = _diff_attention(dq, dk, dv, rel_bias, lam)
        o_diff = (_rmsnorm(o_d, diff_out_norm_w[layer]) * (1.0 - lam_init)).reshape(b, s, DIFF_WIDTH)

        x = x + jnp.concatenate([o_mla, o_diff], axis=-1) @ w_out[layer]

        h = _rmsnorm(x, mlp_norm_w[layer])
        x = x + jnp.square(jax.nn.relu(h @ w_up[layer])) @ w_down[layer]
    return x
```

```python
import math
from contextlib import ExitStack

import numpy as np
import ml_dtypes

import concourse.bass as bass
import concourse.mybir as mybir
from concourse.bass_utils import run_bass_kernel_spmd

F32 = mybir.dt.float32
BF16 = mybir.dt.bfloat16
AF = mybir.ActivationFunctionType
ALU = mybir.AluOpType
AX = mybir.AxisListType

D = 1024
INW = 1984
DFF = 4096
EPS = 1e-6
LAM_INIT = 0.8 - 0.6 * math.exp(-0.3 * 0)
SEM_ROLL = 8000
DEBUG_SIM = False
STRICT_SAME_ENGINE = False
A_INTERLEAVE = True
A_ACTCOPY = False
import os
F_HB = os.environ.get('F_HB', '1') == '1'
F_CN = os.environ.get('F_CN', '0') == '1'
F_KS = os.environ.get('F_KS', '1') == '1'
STRIP_W = 1408
DMAX = 640

R_MLP, NR = 0, 1024
C_NEG, C_POS, C_DOUT, C_ATTN, C_QA, C_KVA, C_MQN, C_MQR, C_MKN, C_MKR, C_DQW, C_DKW, NCOL = 0, 4, 8, 9, 17, 19, 20, 21, 22, 23, 24, 25, 32


class Buf:
    __slots__ = ("name", "writes", "reads")

    def __init__(self, name):
        self.name = name
        self.writes = {}
        self.reads = {}


class Sched:
    def __init__(self, nc, stack):
        self.nc = nc
        self.stack = stack
        self.engs = {"pe": nc.tensor, "act": nc.scalar, "dve": nc.vector,
                     "pool": nc.gpsimd, "sp": nc.sync}
        self.sem = {}
        self.cnt = {}
        self.nsem = 0
        self.dsems = []
        for k in self.engs:
            self._new_sem(k)
        self.waited = {}
        self.n_wait = 0
        self.n_ins = 0
        self.prog = {k: [] for k in self.engs}
        self.pend = {k: [] for k in self.engs}

    def _new_sem(self, k):
        self.nsem += 1
        self.sem[k] = self.stack.enter_context(self.nc.semaphore(f"s{self.nsem}_{k}"))
        self.cnt[k] = 0

    def new_dma_sem(self, name):
        self.nsem += 1
        d = [self.stack.enter_context(self.nc.semaphore(f"d{self.nsem}_{name}")), 0]
        self.dsems.append(d)
        return d

    def _wait(self, eng, tok):
        sem, val = tok
        if val <= 0:
            return
        key = (eng, sem.num)
        if self.waited.get(key, 0) >= val:
            return
        self.waited[key] = val
        self.engs[eng].wait_ge(sem, val)
        self.pend[eng].append((sem.num, val))
        self.n_wait += 1

    def _hazards(self, eng, reads, writes):
        for b in reads:
            for k, t in b.writes.items():
                if k == eng and eng in ("pe", "sp"):
                    continue
                self._wait(eng, t)
        for b in writes:
            for k, t in b.reads.items():
                if k != eng or (STRICT_SAME_ENGINE and eng not in ("pe", "sp")):
                    self._wait(eng, t)
            for k, t in b.writes.items():
                if k != eng or (STRICT_SAME_ENGINE and eng not in ("pe", "sp")):
                    self._wait(eng, t)

    def op(self, eng, fn, reads=(), writes=(), inc=True):
        self._hazards(eng, reads, writes)
        ins = fn(self.engs[eng])
        self.n_ins += 1
        if self.cnt[eng] >= SEM_ROLL:
            self._new_sem(eng)
        self.prog[eng].append((self.pend[eng], (self.sem[eng].num, 1) if inc else None, self.n_ins))
        self.pend[eng] = []
        if inc:
            self.cnt[eng] += 1
            ins.then_inc(self.sem[eng], 1)
            tok = (self.sem[eng], self.cnt[eng])
        else:
            tok = (self.sem[eng], self.cnt[eng] + 1)
        for b in reads:
            b.reads[eng] = tok
        for b in writes:
            b.writes = {eng: tok}
            b.reads = {}
        return tok

    def dma(self, q, out, in_, dsem, reads=(), writes=(), **kw):
        self._hazards(q, reads, writes)
        ins = self.engs[q].dma_start(out=out, in_=in_, **kw)
        self.n_ins += 1
        self.prog[q].append((self.pend[q], (dsem[0].num, 16), self.n_ins))
        self.pend[q] = []
        dsem[1] += 16
        ins.then_inc(dsem[0], 16)
        tok = (dsem[0], dsem[1])
        key = "dma%d" % dsem[0].num
        for b in reads:
            b.reads[key] = tok
        for b in writes:
            b.writes = {key: tok}
            b.reads = {}
        return tok

    def barrier(self, engines=("pe", "act", "dve", "pool", "sp"), skip=()):
        for e in engines:
            for f in self.engs:
                if f != e and self.cnt[f] > 0:
                    self._wait(e, (self.sem[f], self.cnt[f]))
            for d in self.dsems:
                if any(d is x for x in skip):
                    continue
                self._wait(e, (d[0], d[1]))


def simulate(S_):
    sem = {}
    pc = {k: 0 for k in S_.prog}
    progress = True
    while progress:
        progress = False
        for k, prog in S_.prog.items():
            while pc[k] < len(prog):
                waits, inc, idx = prog[pc[k]]
                if all(sem.get(n, 0) >= v for n, v in waits):
                    if inc:
                        sem[inc[0]] = sem.get(inc[0], 0) + inc[1]
                    pc[k] += 1
                    progress = True
                else:
                    break
    stuck = {k: (pc[k], len(p)) for k, p in S_.prog.items() if pc[k] < len(p)}
    for k in stuck:
        waits, inc, idx = S_.prog[k][pc[k]]
        print("STUCK", k, "at", pc[k], "of", len(S_.prog[k]), "ins#", idx, "waits", [(n, v, sem.get(n, 0)) for n, v in waits])
    return not stuck


class Carver:
    def __init__(self, t, lo, hi):
        self.t, self.lo, self.hi, self.p = t, lo, hi, lo

    def reset(self):
        self.p = self.lo

    def take(self, n, dt):
        nb = n * 2 if dt == F32 else n
        self.p = (self.p + 15) // 16 * 16
        a = self.t[:, self.p:self.p + nb]
        self.p += nb
        assert self.p <= self.hi, ("carver overflow", self.p, self.hi)
        return a.bitcast(F32) if dt == F32 else a


def build_nc(S, NSEQ):
    NT = S // 128
    NQC = S // 512
    NKB = S // 128
    nc = bass.Bass("TRN2", target_bir_lowering=False)

    def din(name, shape, dt=F32):
        return nc.dram_tensor(name, shape, dt, kind="ExternalInput").ap()

    x_d = din("x", [NSEQ, S, D])
    w_in_d = din("w_in", [D, INW])
    w_uq_d = din("w_uq", [256, 768])
    w_ukv_d = din("w_ukv", [128, 1024])
    w_out_d = din("w_out", [D, D])
    w_up_d = din("w_up", [D, DFF])
    w_dn_d = din("w_down", [DFF, D])
    rows_d = din("rows", [128, NR])
    cols_d = din("cols", [128, NCOL])
    lamb_d = din("lamb", [128, 256])
    cs_d = din("cs", [S, 128])
    strips_d = din("strips", [128, 4 * STRIP_W])
    ident_d = din("ident", [128, 128], BF16)
    out_d = nc.dram_tensor("out", [NSEQ, S, D], F32, kind="ExternalOutput").ap()
    wupb = nc.dram_tensor("wupb", [16, 128, 8 * 256], BF16, kind="Internal").ap()
    wdnb = nc.dram_tensor("wdnb", [DFF, D], BF16, kind="Internal").ap()

    REG_A = 8 * S
    REG_B = max(8 * S, 32768)
    WORK_N = 24320
    with ExitStack() as st:
        S_ = Sched(nc, st)
        sb = lambda name, shape, dt: st.enter_context(nc.sbuf_tensor("sb_" + name, shape, dt))
        REG = sb("reg", [128, REG_A + REG_B], BF16)
        WORK = sb("work", [128, WORK_N], BF16)
        WPB = sb("wpb", [128, 8 * 1536], BF16)
        cols = sb("cols", [128, NCOL], F32)
        lamt = sb("lamt", [128, 384], F32)
        lams = sb("lams", [128, 16], F32)
        ident = sb("ident", [128, 128], BF16)
        ones_b = sb("ones_b", [128, 128], BF16)
        ones_f = sb("ones_f", [128, 128], F32)
        wuq = sb("wuq", [128, 2, 768], BF16)
        wukv = sb("wukv", [128, 1024], BF16)
        ps = st.enter_context(nc.psum_tensor("ps", [128, 8, 512], F32))
        bps = [Buf("ps%d" % i) for i in range(8)]

        def psb(bank):
            return ps[:, bank, :].bitcast(BF16)

        b_const = Buf("const")
        b_lam = Buf("lam")
        b_wpb = Buf("wpb")
        b_wsmall = Buf("wsmall")
        b_scr = Buf("scratch_w")

        d_const = S_.new_dma_sem("const")
        d_wsm = S_.new_dma_sem("wsm")
        d_scr = S_.new_dma_sem("scr")
        d_wp = [S_.new_dma_sem("wp0"), S_.new_dma_sem("wp1")]
        d_x = [S_.new_dma_sem("x0"), S_.new_dma_sem("x1")]
        d_cs = [S_.new_dma_sem("cs%d" % i) for i in range(4)]
        d_out = [[S_.new_dma_sem("out%d_%d" % (k, j)) for j in range(4)] for k in range(2)]
        d_strip = S_.new_dma_sem("strip")
        d_wo = S_.new_dma_sem("wo")
        d_wu = [S_.new_dma_sem("wu%d" % i) for i in range(3)]
        d_wd = [S_.new_dma_sem("wd%d" % i) for i in range(3)]
        d_xc = [[S_.new_dma_sem("xc%d_%d" % (k, j)) for j in range(4)] for k in range(2)]

        QN = REG[:, 0:4 * S].rearrange("p (h s) -> p h s", h=4)
        KN = REG[:, 4 * S:8 * S].rearrange("p (h s) -> p h s", h=4)
        DQ = KN
        o2 = REG_A
        VT = REG[:, o2:o2 + 4 * S].rearrange("p (t n) -> p t n", n=512)
        DK = REG[:, o2:o2 + 4 * S].rearrange("p (h s) -> p h s", h=4)
        QR = REG[:, o2 + 4 * S:o2 + 6 * S].rearrange("p (h s) -> p h s", h=2)
        KR = REG[:, o2 + 6 * S:o2 + 8 * S].rearrange("p (h s) -> p h s", h=2)
        DVT = REG[:, o2 + 4 * S:o2 + 8 * S].rearrange("p (t n) -> p t n", n=512)
        bQ = [[Buf("q%d_%d" % (h, c)) for c in range(NQC)] for h in range(4)]
        bK = [[Buf("k%d_%d" % (h, c)) for c in range(NQC)] for h in range(4)]
        bV = Buf("v")
        bR3 = Buf("r3")

        def all_region_bufs():
            return [b for r in bQ for b in r] + [b for r in bK for b in r] + [bV, bR3]

        S_.dma("sp", ident[:], ident_d, d_const, writes=[b_const])
        S_.dma("sp", cols[:], cols_d, d_const, writes=[b_const])
        S_.dma("sp", lamt[:, 0:256], lamb_d, d_const, writes=[b_const])
        stg_q = WORK[:, 0:3072].bitcast(F32).rearrange("p (c n) -> p c n", c=2)
        stg_kv = WORK[:, 3072:5120].bitcast(F32)
        b_stg = Buf("stg")
        S_.dma("sp", stg_q, w_uq_d.rearrange("(c p) n -> p c n", p=128), d_wsm, writes=[b_stg])
        S_.dma("sp", stg_kv, w_ukv_d, d_wsm, writes=[b_stg])
        for c in range(2):
            S_.op("dve", lambda e: e.tensor_scalar(out=wuq[:, c, :], in0=stg_q[:, c, :], scalar1=cols[:, C_QA + c:C_QA + c + 1],
                                                   scalar2=None, op0=ALU.mult), reads=[b_stg, b_const], writes=[b_wsmall])
        S_.op("dve", lambda e: e.tensor_scalar(out=wukv[:], in0=stg_kv, scalar1=cols[:, C_KVA:C_KVA + 1], scalar2=None, op0=ALU.mult),
              reads=[b_stg, b_const], writes=[b_wsmall])
        S_.op("dve", lambda e: e.memset(ones_b[:], 1.0), writes=[b_lam])
        S_.op("dve", lambda e: e.memset(ones_f[:], 1.0), writes=[b_lam])
        lv = lamt[:, 0:256].rearrange("p (a b) -> p a b", a=2)
        lprod = lamt[:, 256:384].rearrange("p (a b) -> p a b", a=2)
        S_.op("dve", lambda e: e.tensor_tensor(out=lprod, in0=lv[:, :, 0:64], in1=lv[:, :, 64:128], op=ALU.mult),
              reads=[b_const], writes=[b_const])
        S_.op("dve", lambda e: e.reduce_sum(out=lams[:, 0:2], in_=lprod, axis=AX.X), reads=[b_const], writes=[b_lam])
        S_.op("act", lambda e: e.activation(out=lams[:, 2:4], in_=lams[:, 0:2], func=AF.Exp), reads=[b_lam], writes=[b_lam])
        S_.op("dve", lambda e: e.tensor_tensor(out=lams[:, 4:5], in0=lams[:, 2:3], in1=lams[:, 3:4], op=ALU.subtract),
              reads=[b_lam], writes=[b_lam])
        S_.op("dve", lambda e: e.tensor_scalar(out=lams[:, 5:6], in0=lams[:, 4:5], scalar1=LAM_INIT, scalar2=-1.0,
                                               op0=ALU.add, op1=ALU.mult), reads=[b_lam], writes=[b_lam])
        S_.op("dve", lambda e: e.tensor_scalar(out=lams[:, 6:7], in0=cols[:, C_DOUT:C_DOUT + 1], scalar1=1.0 - LAM_INIT,
                                               scalar2=None, op0=ALU.mult), reads=[b_const, b_lam], writes=[b_lam])
        neglam = lams[:, 5:6]
        doutc = lams[:, 6:7]
        for (dst, a, b) in ((7, C_MQN, C_MKN), (8, C_MQR, C_MKR), (9, C_DQW, C_DKW)):
            S_.op("dve", lambda e: e.tensor_tensor(out=lams[:, dst:dst + 1], in0=cols[:, a:a + 1], in1=cols[:, b:b + 1], op=ALU.mult),
                  reads=[b_const, b_lam], writes=[b_lam])
        wqk_n, wqk_r, wdqk = lams[:, 7:8], lams[:, 8:9], lams[:, 9:10]
        wup_src = w_up_d.rearrange("(c p) (g n) -> g p c n", p=128, n=256)
        wup_dst = wupb.rearrange("g p (c n) -> g p c n", n=256)
        for g in range(16):
            S_.dma("pool", wup_dst[g], wup_src[g], d_scr, writes=[b_scr])
        for g in range(8):
            S_.dma("pool", wdnb[g * 512:(g + 1) * 512, :], w_dn_d[g * 512:(g + 1) * 512, :], d_scr, writes=[b_scr])

        WK = Carver(WORK, 0, WORK_N)
        S_.barrier(skip=[d_scr])

        def fold_wp(mla, lo):
            ncol = 448 if mla else 1536
            WP = WPB[:, 0:8 * ncol].rearrange("p (c n) -> p c n", c=8)
            src = w_in_d[:, 0:448] if mla else w_in_d[:, 448:INW]
            stg = [WORK[:, lo + i * 3072:lo + (i + 1) * 3072].bitcast(F32) for i in range(2)]
            bstg = [Buf("wstg0"), Buf("wstg1")]
            for c in range(8):
                S_.dma("sp", stg[c % 2][:, 0:ncol], src[c * 128:(c + 1) * 128, :], d_wp[c % 2], writes=[bstg[c % 2]])
                S_.op("dve", lambda e: e.tensor_scalar(out=WP[:, c, :], in0=stg[c % 2][:, 0:ncol], scalar1=cols[:, C_ATTN + c:C_ATTN + c + 1],
                                                       scalar2=None, op0=ALU.mult), reads=[bstg[c % 2], b_const], writes=[b_wpb])

        def stage_A(s, mla, prefolded=False):
            ncol = 448 if mla else 1536
            WP = WPB[:, 0:8 * ncol].rearrange("p (c n) -> p c n", c=8)
            if not prefolded:
                fold_wp(mla, 0)
                S_.barrier(skip=[d_scr])
            WK.reset()
            junk = WK.take(1024, BF16); bjunk = Buf("junk")
            sets = []
            NS = 4
            xts = [WK.take(1024, F32) for _ in range(2)]
            bxs = [Buf("xt0"), Buf("xt1")]
            for k in range(NS):
                d = dict(k=k)
                d["cst"] = WK.take(128, F32) if mla else None; d["bcs"] = Buf("cs%d" % k)
                d["A"] = WK.take(1024, BF16); d["bA"] = Buf("A%d" % k)
                d["B"] = WK.take(1024, BF16); d["bB"] = Buf("B%d" % k)
                d["stt"] = WK.take(64, F32); d["bst"] = Buf("st%d" % k)
                d["sq"] = WK.take(768 if mla else 1024, F32); d["bsq"] = Buf("sq%d" % k)
                d["tmp2"] = WK.take(256, F32) if mla else None; d["btmp2"] = Buf("tmp2%d" % k)
                d["krot"] = WK.take(64, F32) if mla else None; d["bkrot"] = Buf("krot%d" % k)
                sets.append(d)

            def tile_gen(t, d):
                k = d["k"]
                cst, A, B, stt, sq, tmp2, krot = (d[n] for n in ("cst", "A", "B", "stt", "sq", "tmp2", "krot"))
                bcs, bA, bB, bst, bsq, btmp2, bkrot = (d[n] for n in ("bcs", "bA", "bB", "bst", "bsq", "btmp2", "bkrot"))
                xt, bx = xts[t % 2], bxs[t % 2]
                if mla:
                    pT = pP = pY = 2 * k
                    pX = 2 * k + 1
                else:
                    pT = pP = pY = 2 * k
                    pX = 2 * k + 1
                tok = slice(t * 128, (t + 1) * 128)
                hT3 = B.rearrange("p (c t) -> p c t", c=8)

                def rstd_from_ms(dst, src_):
                    S_.op("act", lambda e: e.activation(out=dst, in_=src_, func=AF.Ln, bias=EPS), reads=[bst], writes=[bst])
                    S_.op("act", lambda e: e.activation(out=dst, in_=dst, func=AF.Exp, scale=-0.5), reads=[bst], writes=[bst])

                def rope(dst, src_, nh, rd, wr):
                    cos = cst[:, 0:64].unsqueeze(1).broadcast_to([128, nh, 64])
                    s_lo = cst[:, 64:96].unsqueeze(1).broadcast_to([128, nh, 32])
                    s_hi = cst[:, 96:128].unsqueeze(1).broadcast_to([128, nh, 32])
                    t2 = tmp2[:, 0:nh * 64].rearrange("p (h d) -> p h d", h=nh)
                    S_.op("dve", lambda e: e.tensor_tensor(out=t2[:, :, 0:32], in0=src_[:, :, 32:64], in1=s_lo, op=ALU.mult),
                          reads=rd + [bcs], writes=[btmp2])
                    S_.op("dve", lambda e: e.tensor_tensor(out=t2[:, :, 32:64], in0=src_[:, :, 0:32], in1=s_hi, op=ALU.mult),
                          reads=rd + [bcs], writes=[btmp2])
                    S_.op("dve", lambda e: e.tensor_tensor(out=dst, in0=src_, in1=cos, op=ALU.mult),
                          reads=rd + [bcs], writes=wr)
                    S_.op("dve", lambda e: e.tensor_tensor(out=dst, in0=dst, in1=t2, op=ALU.add),
                          reads=wr + [btmp2], writes=wr)

                S_.dma("sp", xt, x_d[s, tok, :], d_x[t % 2], writes=[bx])
                if mla:
                    S_.dma("sp", cst, cs_d[tok, :], d_cs[k], writes=[bcs])
                yield
                S_.op("act", lambda e: e.activation(out=junk, in_=xt, func=AF.Square, scale=1.0 / 32.0,
                                                    accum_out=stt[:, 0:1]), reads=[bx], writes=[bjunk, bst])
                rstd_from_ms(stt[:, 1:2], stt[:, 0:1])
                yield
                if F_HB:
                    S_.op("act", lambda e: e.activation(out=A, in_=xt, func=AF.Copy, scale=stt[:, 1:2]),
                          reads=[bx, bst], writes=[bA])
                else:
                    S_.op("dve", lambda e: e.tensor_scalar(out=A, in0=xt, scalar1=stt[:, 1:2], scalar2=None, op0=ALU.mult),
                          reads=[bx, bst], writes=[bA])
                yield
                for c in range(8):
                    S_.op("pe", lambda e: e.transpose(out=psb(pT)[:, c * 128:(c + 1) * 128], in_=A[:, c * 128:(c + 1) * 128],
                                                      identity=ident[:]), reads=[bA, b_const], writes=[bps[pT]], inc=(c == 7))
                yield
                if A_ACTCOPY:
                    S_.op("act", lambda e: e.activation(out=B, in_=psb(pT), func=AF.Copy), reads=[bps[pT]], writes=[bB])
                else:
                    S_.op("dve", lambda e: e.tensor_copy(out=B, in_=psb(pT)), reads=[bps[pT]], writes=[bB])
                yield
                if mla:
                    for c in range(8):
                        S_.op("pe", lambda e: e.matmul(ps[:, pP, 0:448], lhsT=hT3[:, c, :], rhs=WP[:, c, 0:448],
                                                       start=(c == 0), stop=(c == 7)),
                              reads=[bB, b_wpb], writes=[bps[pP]], inc=(c == 7))
                else:
                    for nb, bank in ((2, pY), (0, pX)):
                        for c in range(8):
                            S_.op("pe", lambda e: e.matmul(ps[:, bank, :], lhsT=hT3[:, c, :], rhs=WP[:, c, nb * 512:(nb + 1) * 512],
                                                           start=(c == 0), stop=(c == 7)),
                                  reads=[bB, b_wpb], writes=[bps[bank]], inc=(c == 7))
                yield
                if mla:
                    cn = A
                    S_.op("act", lambda e: e.activation(out=junk[:, 0:256], in_=ps[:, pP, 0:256], func=AF.Square, scale=1.0 / 16.0,
                                                        accum_out=stt[:, 2:3]), reads=[bps[pP]], writes=[bjunk, bst])
                    S_.op("act", lambda e: e.activation(out=junk[:, 0:128], in_=ps[:, pP, 256:384], func=AF.Square,
                                                        scale=128.0 ** -0.5, accum_out=stt[:, 3:4]), reads=[bps[pP]], writes=[bjunk, bst])
                    rstd_from_ms(stt[:, 4:6], stt[:, 2:4])
                    S_.op("act", lambda e: e.activation(out=junk[:, 0:64], in_=ps[:, pP, 384:448], func=AF.Square, scale=192.0 ** -0.5,
                                                        accum_out=stt[:, 24:25]), reads=[bps[pP]], writes=[bjunk, bst])
                    yield
                    if F_CN:
                        S_.op("act", lambda e: e.activation(out=cn[:, 0:256], in_=ps[:, pP, 0:256], func=AF.Copy, scale=stt[:, 4:5]),
                              reads=[bps[pP], bst], writes=[bA])
                        S_.op("act", lambda e: e.activation(out=cn[:, 256:384], in_=ps[:, pP, 256:384], func=AF.Copy, scale=stt[:, 5:6]),
                              reads=[bps[pP], bst], writes=[bA])
                    else:
                        S_.op("dve", lambda e: e.tensor_scalar(out=cn[:, 0:256], in0=ps[:, pP, 0:256], scalar1=stt[:, 4:5], scalar2=None, op0=ALU.mult),
                              reads=[bps[pP], bst], writes=[bA])
                        S_.op("dve", lambda e: e.tensor_scalar(out=cn[:, 256:384], in0=ps[:, pP, 256:384], scalar1=stt[:, 5:6], scalar2=None, op0=ALU.mult),
                              reads=[bps[pP], bst], writes=[bA])
                    kr3 = krot.rearrange("p (h d) -> p h d", h=1)
                    rope(kr3, ps[:, pP, 384:448].rearrange("p (h d) -> p h d", h=1), 1, [bps[pP]], [bkrot])
                    yield
                    for c in range(3):
                        S_.op("pe", lambda e: e.transpose(out=psb(pT)[:, c * 128:(c + 1) * 128], in_=cn[:, c * 128:(c + 1) * 128],
                                                          identity=ident[:]), reads=[bA, b_const], writes=[bps[pT]], inc=(c == 2))
                    yield
                    S_.op("dve", lambda e: e.tensor_copy(out=B[:, 0:384], in_=psb(pT)[:, 0:384]), reads=[bps[pT]], writes=[bB])
                    cT3 = B[:, 0:384].rearrange("p (c t) -> p c t", c=3)
                    yield
                    for (bank, c0, w) in ((pX, 0, 512), (pY, 512, 256)):
                        for c in range(2):
                            S_.op("pe", lambda e: e.matmul(ps[:, bank, 0:w], lhsT=cT3[:, c, :], rhs=wuq[:, c, c0:c0 + w],
                                                           start=(c == 0), stop=(c == 1)),
                                  reads=[bB, b_wsmall], writes=[bps[bank]], inc=(c == 1))
                    yield
                    S_.op("act", lambda e: e.activation(out=sq[:, 0:512], in_=ps[:, pX, :], func=AF.Square, scale=192.0 ** -0.5),
                          reads=[bps[pX]], writes=[bsq])
                    S_.op("act", lambda e: e.activation(out=sq[:, 512:768], in_=ps[:, pY, 0:256], func=AF.Square, scale=192.0 ** -0.5),
                          reads=[bps[pY]], writes=[bsq])
                    yield
                    S_.op("dve", lambda e: e.reduce_sum(out=stt[:, 8:12], in_=sq[:, 0:512].rearrange("p (h d) -> p h d", h=4), axis=AX.X),
                          reads=[bsq], writes=[bst])
                    S_.op("dve", lambda e: e.reduce_sum(out=stt[:, 12:16], in_=sq[:, 512:768].rearrange("p (h d) -> p h d", h=4), axis=AX.X),
                          reads=[bsq], writes=[bst])
                    S_.op("dve", lambda e: e.tensor_tensor(out=stt[:, 16:20], in0=stt[:, 8:12], in1=stt[:, 12:16], op=ALU.add),
                          reads=[bst], writes=[bst])
                    yield
                    rstd_from_ms(stt[:, 20:24], stt[:, 16:20])
                    yield
                    rq_n = stt[:, 20:24].unsqueeze(2).broadcast_to([128, 4, 128])
                    rq_r = stt[:, 20:24].unsqueeze(2).broadcast_to([128, 4, 64])
                    Qtok = A
                    S_.op("dve", lambda e: e.tensor_tensor(out=Qtok[:, 0:512].rearrange("p (h d) -> p h d", h=4),
                                                           in0=ps[:, pX, :].rearrange("p (h d) -> p h d", h=4), in1=rq_n, op=ALU.mult),
                          reads=[bps[pX], bst], writes=[bA])
                    qr3 = sq[:, 0:256].rearrange("p (h d) -> p h d", h=4)
                    rope(qr3, ps[:, pY, 0:256].rearrange("p (h d) -> p h d", h=4), 4, [bps[pY]], [bsq])
                    S_.op("dve", lambda e: e.tensor_tensor(out=Qtok[:, 512:768].rearrange("p (h d) -> p h d", h=4), in0=qr3, in1=rq_r,
                                                           op=ALU.mult), reads=[bsq, bst], writes=[bA])
                    yield
                    for c in range(6):
                        S_.op("pe", lambda e: e.transpose(out=psb(pT)[:, c * 128:(c + 1) * 128], in_=Qtok[:, c * 128:(c + 1) * 128],
                                                          identity=ident[:]), reads=[bA, b_const], writes=[bps[pT]], inc=(c == 5))
                    yield
                    qcb = [bQ[h][t // 4] for h in range(4)]
                    if A_ACTCOPY:
                        S_.op("act", lambda e: e.activation(out=QN[:, :, tok], in_=psb(pT)[:, 0:512].rearrange("p (h t) -> p h t", h=4),
                                                            func=AF.Copy), reads=[bps[pT]], writes=qcb)
                        S_.op("act", lambda e: e.activation(out=QR[:, :, tok], in_=psb(pT)[:, 512:768].rearrange("p (h t) -> p h t", h=2),
                                                            func=AF.Copy), reads=[bps[pT]], writes=[bR3])
                    else:
                        S_.op("dve", lambda e: e.tensor_copy(out=QN[:, :, tok], in_=psb(pT)[:, 0:512].rearrange("p (h t) -> p h t", h=4)),
                              reads=[bps[pT]], writes=qcb)
                        S_.op("dve", lambda e: e.tensor_copy(out=QR[:, :, tok], in_=psb(pT)[:, 512:768].rearrange("p (h t) -> p h t", h=2)),
                              reads=[bps[pT]], writes=[bR3])
                    yield
                    for (bank, c0) in ((pX, 0), (pY, 512)):
                        S_.op("pe", lambda e: e.matmul(ps[:, bank, :], lhsT=cT3[:, 2, :], rhs=wukv[:, c0:c0 + 512], start=True, stop=True),
                              reads=[bB, b_wsmall], writes=[bps[bank]])
                    yield
                    S_.op("act", lambda e: e.activation(out=sq[:, 0:512], in_=ps[:, pX, :], func=AF.Square, scale=192.0 ** -0.5),
                          reads=[bps[pX]], writes=[bsq])
                    S_.op("act", lambda e: e.activation(out=VT[:, t, :], in_=ps[:, pY, :], func=AF.Copy), reads=[bps[pY]], writes=[bV])
                    yield
                    S_.op("dve", lambda e: e.reduce_sum(out=stt[:, 8:12], in_=sq[:, 0:512].rearrange("p (h d) -> p h d", h=4), axis=AX.X),
                          reads=[bsq], writes=[bst])
                    S_.op("dve", lambda e: e.tensor_scalar(out=stt[:, 16:20], in0=stt[:, 8:12], scalar1=stt[:, 24:25], scalar2=None,
                                                           op0=ALU.add), reads=[bst], writes=[bst])
                    yield
                    rstd_from_ms(stt[:, 28:32], stt[:, 16:20])
                    yield
                    rk_n = stt[:, 28:32].unsqueeze(2).broadcast_to([128, 4, 128])
                    rk_r = stt[:, 28:32].unsqueeze(2).broadcast_to([128, 4, 64])
                    Ktok = A
                    S_.op("dve", lambda e: e.tensor_tensor(out=Ktok[:, 0:512].rearrange("p (h d) -> p h d", h=4),
                                                           in0=ps[:, pX, :].rearrange("p (h d) -> p h d", h=4), in1=rk_n, op=ALU.mult),
                          reads=[bps[pX], bst], writes=[bA])
                    S_.op("dve", lambda e: e.tensor_tensor(out=Ktok[:, 512:768].rearrange("p (h d) -> p h d", h=4),
                                                           in0=krot.unsqueeze(1).broadcast_to([128, 4, 64]), in1=rk_r, op=ALU.mult),
                          reads=[bkrot, bst], writes=[bA])
                    yield
                    for c in range(6):
                        S_.op("pe", lambda e: e.transpose(out=psb(pT)[:, c * 128:(c + 1) * 128], in_=Ktok[:, c * 128:(c + 1) * 128],
                                                          identity=ident[:]), reads=[bA, b_const], writes=[bps[pT]], inc=(c == 5))
                    yield
                    kcb = [bK[h][t // 4] for h in range(4)]
                    S_.op("dve", lambda e: e.tensor_scalar(out=KN[:, :, tok], in0=psb(pT)[:, 0:512].rearrange("p (h t) -> p h t", h=4),
                                                           scalar1=wqk_n, scalar2=None, op0=ALU.mult),
                          reads=[bps[pT], b_lam], writes=kcb)
                    S_.op("dve", lambda e: e.tensor_scalar(out=KR[:, :, tok], in0=psb(pT)[:, 512:768].rearrange("p (h t) -> p h t", h=2),
                                                           scalar1=wqk_r, scalar2=None, op0=ALU.mult),
                          reads=[bps[pT], b_lam], writes=[bR3])
                    yield
                else:
                    cn = A
                    S_.op("act", lambda e: e.activation(out=DVT[:, t, :], in_=ps[:, pY, :], func=AF.Copy), reads=[bps[pY]], writes=[bR3])
                    for (which, s0, r0, dst) in ((0, 8, 32, 0), (1, 16, 40, 512)):
                        if which == 1:
                            for c in range(8):
                                S_.op("pe", lambda e: e.matmul(ps[:, pX, :], lhsT=hT3[:, c, :], rhs=WP[:, c, 512:1024],
                                                               start=(c == 0), stop=(c == 7)),
                                      reads=[bB, b_wpb], writes=[bps[pX]], inc=(c == 7))
                            yield
                        sqh = sq[:, dst:dst + 512]
                        S_.op("act", lambda e: e.activation(out=sqh, in_=ps[:, pX, :], func=AF.Square, scale=0.125),
                              reads=[bps[pX]], writes=[bsq])
                        yield
                        S_.op("dve", lambda e: e.reduce_sum(out=stt[:, s0:s0 + 8], in_=sqh.rearrange("p (h d) -> p h d", h=8), axis=AX.X),
                              reads=[bsq], writes=[bst])
                        yield
                        rstd_from_ms(stt[:, r0:r0 + 8], stt[:, s0:s0 + 8])
                        yield
                        S_.op("dve", lambda e: e.tensor_tensor(out=cn[:, dst:dst + 512].rearrange("p (h d) -> p h d", h=8),
                                                               in0=ps[:, pX, :].rearrange("p (h d) -> p h d", h=8),
                                                               in1=stt[:, r0:r0 + 8].unsqueeze(2).broadcast_to([128, 8, 64]), op=ALU.mult),
                              reads=[bps[pX], bst], writes=[bA])
                        yield
                    for c in range(8):
                        S_.op("pe", lambda e: e.transpose(out=psb(pT)[:, c * 128:(c + 1) * 128], in_=cn[:, c * 128:(c + 1) * 128],
                                                          identity=ident[:]), reads=[bA, b_const], writes=[bps[pT]], inc=(c == 7))
                    yield
                    kcb = [bK[h][t // 4] for h in range(4)]
                    S_.op("dve", lambda e: e.tensor_copy(out=DQ[:, :, tok], in_=psb(pT)[:, 0:512].rearrange("p (h t) -> p h t", h=4)),
                          reads=[bps[pT]], writes=kcb)
                    if A_ACTCOPY:
                        S_.op("act", lambda e: e.activation(out=DK[:, :, tok], in_=psb(pT)[:, 512:1024].rearrange("p (h t) -> p h t", h=4),
                                                            func=AF.Copy), reads=[bps[pT]], writes=[bV])
                    else:
                        S_.op("dve", lambda e: e.tensor_scalar(out=DK[:, :, tok], in0=psb(pT)[:, 512:1024].rearrange("p (h t) -> p h t", h=4),
                                                               scalar1=wdqk, scalar2=None, op0=ALU.mult),
                              reads=[bps[pT], b_lam], writes=[bV])
                    yield

            active = []
            nxt = 0
            STAG = 6 if mla else 5
            while nxt < NT or active:
                if nxt < NT and (not active or (A_INTERLEAVE and len(active) < NS and active[-1][1] >= STAG)):
                    d = sets[nxt % NS]
                    active.append([tile_gen(nxt, d), 0])
                    nxt += 1
                for a in list(active):
                    try:
                        next(a[0])
                        a[1] += 1
                    except StopIteration:
                        active.remove(a)

        def stage_B(mla):
            WK.reset()
            NPT = 4
            PT = [WK.take(1024, BF16) for _ in range(NPT)]
            bpt = [Buf("pt%d" % i) for i in range(NPT)]
            QM = [WK.take(512, BF16) for _ in range(2)]
            bqm = [Buf("qm0"), Buf("qm1")]
            tr = WK.take(512, F32); btr = Buf("tr")
            to = WK.take(512, F32); bto = Buf("to")
            tt = WK.take(512, F32); btt = Buf("tt")
            tsq, btsq, trs, btrs = tr, btr, tt, btt
            bstf = Buf("stf")
            if not mla:
                stf = WK.take(4 * STRIP_W, F32)
                BH = WPB[:, 0:4 * STRIP_W].rearrange("p (h w) -> p h w", h=4)
                BL = WPB[:, 4 * STRIP_W:8 * STRIP_W].rearrange("p (h w) -> p h w", h=4)
                S_.dma("sp", stf, strips_d, d_strip, writes=[bstf])
                S_.op("dve", lambda e: e.tensor_scalar(out=stf, in0=stf, scalar1=8.0, scalar2=None, op0=ALU.mult),
                      reads=[bstf], writes=[bstf])
                S_.op("dve", lambda e: e.tensor_copy(out=WPB[:, 0:4 * STRIP_W], in_=stf), reads=[bstf], writes=[b_wpb])
                S_.op("dve", lambda e: e.tensor_tensor(out=WPB[:, 4 * STRIP_W:8 * STRIP_W], in0=stf, in1=WPB[:, 0:4 * STRIP_W],
                                                       op=ALU.subtract), reads=[bstf, b_wpb], writes=[b_wpb])
            bacc = [Buf("acc0"), Buf("acc1")]
            units = []
            if mla:
                for h in range(4):
                    for qc in range(NQC):
                        units.append(dict(h=h, qc=qc, m=0))
            else:
                for h in range(4):
                    for qc in range(NQC):
                        for m in range(2):
                            units.append(dict(h=h, qc=qc, m=m))
            NSL = NKB // 2
            slots = [(ui, j) for ui in range(len(units)) for j in range(NSL)]
            deferred = []
            L = 1

            def prep_qm(ui):
                u = units[ui]
                h, qc, m = u["h"], u["qc"], u["m"]
                qs = slice(qc * 512, (qc + 1) * 512)
                qm = QM[ui % 2]
                if mla:
                    p0 = (h % 2) * 64
                    srcq = QR[p0:p0 + 64, h // 2, qs]
                    rb_ = [bR3]
                else:
                    p0 = m * 64
                    srcq = DQ[p0:p0 + 64, h, qs]
                    rb_ = [bK[h][qc]]
                pd = 64 - p0
                S_.op("pool", lambda e: e.memset(qm[pd:pd + 64, :], 0.0), writes=[bqm[ui % 2]])
                S_.op("pool", lambda e: e.tensor_copy(out=qm[p0:p0 + 64, :], in_=srcq), reads=rb_, writes=[bqm[ui % 2]])

            def bias_kind(qc, kb):
                if mla:
                    return "none"
                if kb < 4 * qc - 1:
                    return "neg"
                if kb > 4 * qc + 4:
                    return "pos"
                return "near"

            def emit_scores(g):
                ui, j = slots[g]
                u = units[ui]
                h, qc, m = u["h"], u["qc"], u["m"]
                qs = slice(qc * 512, (qc + 1) * 512)
                r = g % 2
                if g == 0:
                    prep_qm(0)
                if j == NSL // 2 and ui + 1 < len(units):
                    prep_qm(ui + 1)
                qm = QM[ui % 2]
                for t_ in range(2):
                    kb = 2 * j + t_
                    near = bias_kind(qc, kb) == "near"
                    ks = slice(kb * 128, (kb + 1) * 128)
                    bank = 2 * r + t_
                    last = (t_ == 1)
                    if mla:
                        S_.op("pe", lambda e: e.matmul(ps[:, bank, :], lhsT=KN[:, h, ks], rhs=QN[:, h, qs], start=True, stop=False),
                              reads=[bK[h][kb // 4], bQ[h][qc]], writes=[bps[bank]], inc=False)
                        S_.op("pe", lambda e: e.matmul(ps[:, bank, :], lhsT=KR[:, h // 2, ks], rhs=qm, start=False, stop=True),
                              reads=[bR3, bqm[ui % 2]], writes=[bps[bank]], inc=last)
                    else:
                        S_.op("pe", lambda e: e.matmul(ps[:, bank, :], lhsT=DK[:, h, ks], rhs=qm, start=True, stop=(not near)),
                              reads=[bV, bqm[ui % 2]], writes=[bps[bank]], inc=(not near))
                        if near:
                            off = DMAX - (kb - 4 * qc) * 128
                            S_.op("pe", lambda e: e.matmul(ps[:, bank, :], lhsT=ident[:], rhs=BH[:, h, off:off + 512], start=False, stop=False),
                                  reads=[b_wpb, b_const], writes=[bps[bank]], inc=False)
                            S_.op("pe", lambda e: e.matmul(ps[:, bank, :], lhsT=ident[:], rhs=BL[:, h, off:off + 512], start=False, stop=True),
                                  reads=[b_wpb, b_const], writes=[bps[bank]], inc=True)

            def emit_exp_pv(g):
                ui, j = slots[g]
                u = units[ui]
                h, qc, m = u["h"], u["qc"], u["m"]
                r = g % 2
                pt = PT[g % NPT]
                bp = bpt[g % NPT]
                ob, sbk = (4, 5) if ui % 2 == 0 else (6, 7)
                if mla:
                    scale = 192.0 ** -0.5
                    bias = 0.0
                    vb = bV
                else:
                    scale = 0.125
                    vb = bR3

                def bias_of(kind):
                    if kind == "neg":
                        return cols[:, C_NEG + h:C_NEG + h + 1]
                    if kind == "pos":
                        return cols[:, C_POS + h:C_POS + h + 1]
                    return 0.0
                k0, k1 = bias_kind(qc, 2 * j), bias_kind(qc, 2 * j + 1)
                if k0 == k1:
                    S_.op("act", lambda e: e.activation(out=pt.rearrange("p (b n) -> p b n", b=2), in_=ps[:, 2 * r:2 * r + 2, :],
                                                        func=AF.Exp, scale=scale, bias=bias_of(k0)),
                          reads=[bps[2 * r], bps[2 * r + 1], b_const], writes=[bp])
                else:
                    for t_, kk in ((0, k0), (1, k1)):
                        S_.op("act", lambda e: e.activation(out=pt[:, t_ * 512:(t_ + 1) * 512], in_=ps[:, 2 * r + t_, :],
                                                            func=AF.Exp, scale=scale, bias=bias_of(kk)),
                              reads=[bps[2 * r + t_], b_const], writes=[bp])
                for t_ in range(2):
                    kb = 2 * j + t_
                    vap = (VT if mla else DVT)[:, kb, h * 128:(h + 1) * 128]
                    S_.op("pe", lambda e: e.matmul(ps[:, ob, :], lhsT=vap, rhs=pt[:, t_ * 512:(t_ + 1) * 512],
                                                   start=(kb == 0), stop=(kb == NKB - 1)),
                          reads=[vb, bp], writes=[bacc[ui % 2]], inc=False)
                for t_ in range(2):
                    kb = 2 * j + t_
                    S_.op("pe", lambda e: e.matmul(ps[:, sbk, :], lhsT=ones_b[:], rhs=pt[:, t_ * 512:(t_ + 1) * 512],
                                                   start=(kb == 0), stop=(kb == NKB - 1)),
                          reads=[b_lam, bp], writes=[bacc[ui % 2]], inc=(t_ == 1))
                if j == NSL - 1:
                    finalize(ui, g)

            def finalize(ui, g):
                u = units[ui]
                h, qc, m = u["h"], u["qc"], u["m"]
                ob, sbk = (4, 5) if ui % 2 == 0 else (6, 7)
                ba = bacc[ui % 2]
                qs = slice(qc * 512, (qc + 1) * 512)
                S_.op("dve", lambda e: e.reciprocal(out=tr, in_=ps[:, sbk, :]), reads=[ba], writes=[btr])
                if mla:
                    S_.op("dve", lambda e: e.tensor_tensor(out=QN[:, h, qs], in0=ps[:, ob, :], in1=tr, op=ALU.mult),
                          reads=[ba, btr], writes=[bQ[h][qc]])
                    return
                if m == 0:
                    S_.op("dve", lambda e: e.tensor_tensor(out=to, in0=ps[:, ob, :], in1=tr, op=ALU.mult),
                          reads=[ba, btr], writes=[bto])
                    return
                S_.op("dve", lambda e: e.tensor_tensor(out=tt, in0=ps[:, ob, :], in1=tr, op=ALU.mult),
                      reads=[ba, btr], writes=[btt])
                S_.op("dve", lambda e: e.scalar_tensor_tensor(out=to, in0=tt, scalar=neglam, in1=to, op0=ALU.mult, op1=ALU.add),
                      reads=[btt, bto, b_lam], writes=[bto])
                S_.op("dve", lambda e: e.tensor_tensor(out=tsq, in0=to, in1=to, op=ALU.mult), reads=[bto], writes=[btsq])

                def t1():
                    S_.op("pe", lambda e: e.matmul(ps[:, sbk, :], lhsT=ones_f[:], rhs=tsq, start=True, stop=True),
                          reads=[btsq, b_lam], writes=[ba])

                def t2():
                    S_.op("act", lambda e: e.activation(out=trs, in_=ps[:, sbk, :], func=AF.Ln, scale=1.0 / 128.0, bias=EPS),
                          reads=[ba], writes=[btrs])

                def t3():
                    S_.op("act", lambda e: e.activation(out=trs, in_=trs, func=AF.Exp, scale=-0.5), reads=[btrs], writes=[btrs])
                    S_.op("dve", lambda e: e.scalar_tensor_tensor(out=DQ[:, h, qs], in0=to, scalar=doutc, in1=trs,
                                                                  op0=ALU.mult, op1=ALU.mult),
                          reads=[bto, btrs, b_lam], writes=[bK[h][qc]])
                o3 = min(8, NSL - 1)
                o2 = max(1, o3 - 2)
                o1 = max(1, o3 - 3)
                deferred.append((g + o1, t1))
                deferred.append((g + o2, t2))
                deferred.append((g + o3, t3))

            NG = len(slots)
            for g in range(NG + L):
                if g < NG:
                    emit_scores(g)
                if g >= L:
                    emit_exp_pv(g - L)
                while deferred and deferred[0][0] <= g:
                    deferred.pop(0)[1]()
            while deferred:
                deferred.pop(0)[1]()

        def stage_C(s):
            WK.reset()
            x1s = [[WK.take(1024, F32) for _ in range(4)], None]
            bx1s = [[Buf("x1_%d_%d" % (k, j)) for j in range(4)] for k in range(2)]
            hb2 = [WK.take(1024, BF16) for _ in range(2)]
            bhb2 = [Buf("hb2_0"), Buf("hb2_1")]
            h2T = WK.take(8 * 512, BF16); bh2T = Buf("h2T")
            h2T3 = h2T.rearrange("p (c t) -> p c t", c=8)
            junk = WK.take(1024, BF16); bjunk = Buf("junkc")
            rows = WK.take(1024, F32); b_rows = Buf("rows")
            S_.dma("sp", rows, rows_d, d_strip, writes=[b_rows])
            stt = WK.take(16, F32); bst = [Buf("stc%d" % j) for j in range(4)]
            rl = [WK.take(512, F32) for _ in range(2)]
            brl = [Buf("rl0"), Buf("rl1")]
            o2_ = REG_A
            wout = REG[:, o2_:o2_ + 8192].rearrange("p (c n) -> p c n", c=8); bwout = Buf("wout")
            aT = REG[:, o2_ + 8192:o2_ + 8192 + 16384].rearrange("p (f t) -> p f t", f=32); baT = Buf("aT")
            x1s[1] = [REG[:, o2_ + 24576 + j * 2048:o2_ + 24576 + (j + 1) * 2048].bitcast(F32) for j in range(4)]
            wu = [WPB[:, i * 2048:(i + 1) * 2048].rearrange("p (c n) -> p c n", c=8) for i in range(3)]
            wd = [WPB[:, 6144 + i * 2048:6144 + (i + 1) * 2048].rearrange("p (f n) -> p f n", f=2) for i in range(3)]
            bwu = [Buf("wu%d" % i) for i in range(3)]
            bwd = [Buf("wd%d" % i) for i in range(3)]
            S_.dma("pool", wout, w_out_d.rearrange("(c p) n -> p c n", p=128), d_wo, writes=[bwout])
            wupv = wupb.rearrange("g p (c n) -> g p c n", n=256)
            wdnv = wdnb.rearrange("(g f p) n -> g p f n", f=2, p=128)

            def load_x(ck):
                k = ck % 2
                for j in range(4):
                    tok = slice(ck * 512 + j * 128, ck * 512 + (j + 1) * 128)
                    S_.dma("sp", x1s[k][j], x_d[s, tok, :], d_xc[k][j], writes=[bx1s[k][j]])

            load_x(0)
            gi = 0
            for ck in range(NQC):
                k = ck % 2
                x1, bx1 = x1s[k], bx1s[k]
                rdb = [bQ[h][ck] for h in range(4)] + [bK[h][ck] for h in range(4)] + [bwout]
                for j in range(4):
                    tq = slice(ck * 512 + j * 128, ck * 512 + (j + 1) * 128)
                    for half in range(2):
                        bank = 2 * j + half
                        for c in range(8):
                            lhs = QN[:, c, tq] if c < 4 else DQ[:, c - 4, tq]
                            S_.op("pe", lambda e: e.matmul(ps[:, bank, :], lhsT=lhs, rhs=wout[:, c, half * 512:(half + 1) * 512],
                                                           start=(c == 0), stop=(c == 7)),
                                  reads=rdb, writes=[bps[bank]], inc=(c == 7))
                if ck + 1 < NQC:
                    load_x(ck + 1)
                def p2(j):
                    S_.op("dve", lambda e: e.tensor_tensor(out=x1[j], in0=x1[j], in1=ps[:, 2 * j:2 * j + 2, :].rearrange("p b n -> p (b n)"),
                                                           op=ALU.add), reads=[bx1[j], bps[2 * j], bps[2 * j + 1]], writes=[bx1[j]])
                    S_.op("act", lambda e: e.activation(out=junk, in_=x1[j], func=AF.Square, scale=1.0 / 32.0,
                                                        accum_out=stt[:, 2 * j:2 * j + 1]), reads=[bx1[j]], writes=[bjunk, bst[j]])
                    S_.op("act", lambda e: e.activation(out=stt[:, 2 * j + 1:2 * j + 2], in_=stt[:, 2 * j:2 * j + 1], func=AF.Ln, bias=EPS),
                          reads=[bst[j]], writes=[bst[j]])
                    S_.op("act", lambda e: e.activation(out=stt[:, 2 * j + 1:2 * j + 2], in_=stt[:, 2 * j + 1:2 * j + 2], func=AF.Exp, scale=-0.5),
                          reads=[bst[j]], writes=[bst[j]])
                    S_.op("dve", lambda e: e.scalar_tensor_tensor(out=hb2[j % 2], in0=x1[j], scalar=stt[:, 2 * j + 1:2 * j + 2],
                                                                  in1=rows[:, R_MLP:R_MLP + 1024], op0=ALU.mult, op1=ALU.mult),
                          reads=[bx1[j], bst[j], b_rows], writes=[bhb2[j % 2]])

                def p3(j):
                    bank = 2 * j
                    for c in range(8):
                        S_.op("pe", lambda e: e.transpose(out=psb(bank)[:, c * 128:(c + 1) * 128], in_=hb2[j % 2][:, c * 128:(c + 1) * 128],
                                                          identity=ident[:]), reads=[bhb2[j % 2], b_const], writes=[bps[bank]], inc=(c == 7))
                    S_.op("dve", lambda e: e.tensor_copy(out=h2T3[:, :, j * 128:(j + 1) * 128],
                                                         in_=psb(bank).rearrange("p (c t) -> p c t", c=8)),
                          reads=[bps[bank]], writes=[bh2T])
                p2(0); p2(1); p3(0); p2(2); p3(1); p2(3); p3(2); p3(3)
                for g in range(16):
                    r = (gi + g) % 3
                    S_.dma("pool", wu[r], wupv[g], d_wu[r], reads=[b_scr], writes=[bwu[r]])
                    for fl in range(2):
                        f = g * 2 + fl
                        bank = 1 + 2 * (f % 2)
                        for c in range(8):
                            S_.op("pe", lambda e: e.matmul(ps[:, bank, :], lhsT=wu[r][:, c, fl * 128:(fl + 1) * 128], rhs=h2T3[:, c, :],
                                                           start=(c == 0), stop=(c == 7)),
                                  reads=[bwu[r], bh2T], writes=[bps[bank]], inc=(c == 7))
                        S_.op("act", lambda e: e.activation(out=rl[f % 2], in_=ps[:, bank, :], func=AF.Relu),
                              reads=[bps[bank]], writes=[brl[f % 2]])
                        S_.op("dve", lambda e: e.tensor_tensor(out=aT[:, f, :], in0=rl[f % 2], in1=rl[f % 2], op=ALU.mult),
                              reads=[brl[f % 2]], writes=[baT])
                for g in range(16):
                    r = (gi + g) % 3
                    S_.dma("pool", wd[r], wdnv[g], d_wd[r], reads=[b_scr], writes=[bwd[r]])
                    for fl in range(2):
                        f = g * 2 + fl
                        for j in range(4):
                            for half in range(2):
                                bank = j * 2 + half
                                S_.op("pe", lambda e: e.matmul(ps[:, bank, :], lhsT=aT[:, f, j * 128:(j + 1) * 128],
                                                               rhs=wd[r][:, fl, half * 512:(half + 1) * 512],
                                                               start=(f == 0), stop=(f == 31)),
                                      reads=[baT, bwd[r]], writes=[bps[bank]], inc=(f == 31 or (j == 3 and half == 1)))
                gi += 16
                for j in range(4):
                    tok = slice(ck * 512 + j * 128, ck * 512 + (j + 1) * 128)
                    S_.op("dve", lambda e: e.tensor_tensor(out=x1[j], in0=x1[j],
                                                           in1=ps[:, 2 * j:2 * j + 2, :].rearrange("p b n -> p (b n)"), op=ALU.add),
                          reads=[bx1[j], bps[2 * j], bps[2 * j + 1]], writes=[bx1[j]])
                    S_.dma("sp", out_d[s, tok, :], x1[j], d_out[k][j], reads=[bx1[j]])

        for s in range(NSEQ):
            stage_A(s, True)
            S_.barrier()
            fold_wp(False, WORK_N - 6144)
            stage_B(True)
            S_.barrier()
            stage_A(s, False, prefolded=True)
            S_.barrier()
            stage_B(False)
            S_.barrier()
            stage_C(s)
            S_.barrier()
        S_.barrier(("sp",))
        print("build: n_ins", S_.n_ins, "n_wait", S_.n_wait, "nsem", S_.nsem)
        if DEBUG_SIM:
            print("simulate ok:", simulate(S_))
    return nc


def _t5_bucket_np(rel):
    half = 16
    ret = np.where(rel > 0, half, 0)
    n = np.abs(rel)
    max_exact = half // 2
    large = max_exact + (np.log(np.maximum(n, 1).astype(np.float32) / np.float32(max_exact))
                         / np.float32(math.log(128 / max_exact)) * np.float32(half - max_exact)).astype(np.int32)
    large = np.minimum(large, half - 1)
    return ret + np.where(n < max_exact, n, large)


def _host_tables(S):
    inv = (np.float32(10000.0) ** (-np.arange(0, 64, 2, dtype=np.float32) / np.float32(64))).astype(np.float32)
    ang = np.arange(S, dtype=np.float32)[:, None] * inv[None, :]
    ang = np.concatenate([ang, ang], axis=-1)
    cos = np.cos(ang).astype(np.float32)
    sin = np.sin(ang).astype(np.float32)
    sinS = sin.copy()
    sinS[:, :32] = -sinS[:, :32]
    cs = np.concatenate([cos, sinS], axis=1).astype(np.float32)
    k = np.arange(128)[:, None]
    j = np.arange(STRIP_W)[None, :]
    bucket = _t5_bucket_np((k - j + DMAX).astype(np.int32))
    return cs, bucket


def prepare_inputs(S, x_core_list, p):
    cs, bucket = _host_tables(S)
    f32 = np.float32
    w_uq = np.asarray(p["w_uq"][0], f32)
    perm_q = np.concatenate([np.arange(h * 192, h * 192 + 128) for h in range(4)]
                            + [np.arange(h * 192 + 128, h * 192 + 192) for h in range(4)])
    w_ukv = np.asarray(p["w_ukv"][0], f32)
    perm_kv = np.concatenate([np.arange(h * 256, h * 256 + 128) for h in range(4)]
                             + [np.arange(h * 256 + 128, h * 256 + 256) for h in range(4)])
    row = np.asarray(p["mlp_norm_w"][0], f32)
    assert row.shape[0] == NR
    rows = np.ascontiguousarray(np.broadcast_to(row[None, :], (128, NR)))
    rb = np.asarray(p["rel_bias"], f32)
    cols = np.zeros((128, NCOL), f32)
    cols[:, C_NEG:C_NEG + 4] = rb[15][None, :]
    cols[:, C_POS:C_POS + 4] = rb[31][None, :]
    cols[:, C_DOUT] = np.asarray(p["diff_out_norm_w"][0], f32)
    cols[:, C_ATTN:C_ATTN + 8] = np.asarray(p["attn_norm_w"][0], f32).reshape(8, 128).T
    cols[:, C_QA:C_QA + 2] = np.asarray(p["q_a_norm_w"][0], f32).reshape(2, 128).T
    cols[:, C_KVA] = np.asarray(p["kv_a_norm_w"][0], f32)
    mq = np.asarray(p["mla_q_norm_w"][0], f32); mk = np.asarray(p["mla_k_norm_w"][0], f32)
    cols[:, C_MQN] = mq[0:128]; cols[:, C_MQR] = np.tile(mq[128:192], 2)
    cols[:, C_MKN] = mk[0:128]; cols[:, C_MKR] = np.tile(mk[128:192], 2)
    cols[:, C_DQW] = np.tile(np.asarray(p["diff_q_norm_w"][0], f32), 2)
    cols[:, C_DKW] = np.tile(np.asarray(p["diff_k_norm_w"][0], f32), 2)
    lam = np.concatenate([np.asarray(p[k][0], f32) for k in ("lambda_q1", "lambda_k1", "lambda_q2", "lambda_k2")])
    lamb = np.ascontiguousarray(np.broadcast_to(lam[None, :], (128, 256)))
    strips = np.ascontiguousarray(np.transpose(rb[bucket], (0, 2, 1)).reshape(128, 4 * STRIP_W))
    shared = {
        "w_in": np.ascontiguousarray(p["w_in"][0], dtype=f32),
        "w_uq": np.ascontiguousarray(w_uq[:, perm_q]),
        "w_ukv": np.ascontiguousarray(w_ukv[:, perm_kv]),
        "w_out": np.ascontiguousarray(p["w_out"][0], dtype=f32),
        "w_up": np.ascontiguousarray(p["w_up"][0], dtype=f32),
        "w_down": np.ascontiguousarray(p["w_down"][0], dtype=f32),
        "rows": rows, "cols": cols, "lamb": lamb, "cs": cs, "strips": strips,
        "ident": np.eye(128).astype(ml_dtypes.bfloat16),
    }
    return [dict(shared, x=np.ascontiguousarray(xc, dtype=f32)) for xc in x_core_list]


def kernel(**inputs):
    x = np.asarray(inputs["x"], np.float32)
    B, S, _ = x.shape
    n = 8
    nseq = B // n
    p = {k: np.asarray(v) for k, v in inputs.items() if k != "x"}
    in_maps = prepare_inputs(S, [x[i * nseq:(i + 1) * nseq] for i in range(n)], p)
    nc = build_nc(S, nseq)
    res = run_bass_kernel_spmd(nc, in_maps, core_ids=list(range(n)))
    return np.concatenate([np.asarray(r["out"], np.float32) for r in res.results], axis=0)
```

```python
import math
from contextlib import ExitStack

import numpy as np
import ml_dtypes

import concourse.bass as bass
import concourse.mybir as mybir
from concourse.bass_utils import run_bass_kernel_spmd

F32 = mybir.dt.float32
BF16 = mybir.dt.bfloat16
AF = mybir.ActivationFunctionType
ALU = mybir.AluOpType
AX = mybir.AxisListType

D = 1024
INW = 1984
DFF = 4096
EPS = 1e-6
LAM_INIT = 0.8 - 0.6 * math.exp(-0.3 * 0)
SEM_ROLL = 8000
DEBUG_SIM = False
STRICT_SAME_ENGINE = False
A_INTERLEAVE = True
A_ACTCOPY = False
import os
F_HB = os.environ.get('F_HB', '1') == '1'
F_CN = os.environ.get('F_CN', '0') == '1'
F_KS = os.environ.get('F_KS', '1') == '1'
STRIP_W = 1408
DMAX = 640

R_MLP, NR = 0, 1024
C_NEG, C_POS, C_DOUT, C_ATTN, C_QA, C_KVA, C_MQN, C_MQR, C_MKN, C_MKR, C_DQW, C_DKW, NCOL = 0, 4, 8, 9, 17, 19, 20, 21, 22, 23, 24, 25, 32


class Buf:
    __slots__ = ("name", "writes", "reads")

    def __init__(self, name):
        self.name = name
        self.writes = {}
        self.reads = {}


class Sched:
    def __init__(self, nc, stack):
        self.nc = nc
        self.stack = stack
        self.engs = {"pe": nc.tensor, "act": nc.scalar, "dve": nc.vector,
                     "pool": nc.gpsimd, "sp": nc.sync}
        self.sem = {}
        self.cnt = {}
        self.nsem = 0
        self.dsems = []
        for k in self.engs:
            self._new_sem(k)
        self.waited = {}
        self.n_wait = 0
        self.n_ins = 0
        self.prog = {k: [] for k in self.engs}
        self.pend = {k: [] for k in self.engs}

    def _new_sem(self, k):
        self.nsem += 1
        self.sem[k] = self.stack.enter_context(self.nc.semaphore(f"s{self.nsem}_{k}"))
        self.cnt[k] = 0

    def new_dma_sem(self, name):
        self.nsem += 1
        d = [self.stack.enter_context(self.nc.semaphore(f"d{self.nsem}_{name}")), 0]
        self.dsems.append(d)
        return d

    def _wait(self, eng, tok):
        sem, val = tok
        if val <= 0:
            return
        key = (eng, sem.num)
        if self.waited.get(key, 0) >= val:
            return
        self.waited[key] = val
        self.engs[eng].wait_ge(sem, val)
        self.pend[eng].append((sem.num, val))
        self.n_wait += 1

    def _hazards(self, eng, reads, writes):
        for b in reads:
            for k, t in b.writes.items():
                if k == eng and eng in ("pe", "sp"):
                    continue
                self._wait(eng, t)
        for b in writes:
            for k, t in b.reads.items():
                if k != eng or (STRICT_SAME_ENGINE and eng not in ("pe", "sp")):
                    self._wait(eng, t)
            for k, t in b.writes.items():
                if k != eng or (STRICT_SAME_ENGINE and eng not in ("pe", "sp")):
                    self._wait(eng, t)

    def op(self, eng, fn, reads=(), writes=(), inc=True):
        self._hazards(eng, reads, writes)
        ins = fn(self.engs[eng])
        self.n_ins += 1
        if self.cnt[eng] >= SEM_ROLL:
            self._new_sem(eng)
        self.prog[eng].append((self.pend[eng], (self.sem[eng].num, 1) if inc else None, self.n_ins))
        self.pend[eng] = []
        if inc:
            self.cnt[eng] += 1
            ins.then_inc(self.sem[eng], 1)
            tok = (self.sem[eng], self.cnt[eng])
        else:
            tok = (self.sem[eng], self.cnt[eng] + 1)
        for b in reads:
            b.reads[eng] = tok
        for b in writes:
            b.writes = {eng: tok}
            b.reads = {}
        return tok

    def dma(self, q, out, in_, dsem, reads=(), writes=(), **kw):
        self._hazards(q, reads, writes)
        ins = self.engs[q].dma_start(out=out, in_=in_, **kw)
        self.n_ins += 1
        self.prog[q].append((self.pend[q], (dsem[0].num, 16), self.n_ins))
        self.pend[q] = []
        dsem[1] += 16
        ins.then_inc(dsem[0], 16)
        tok = (dsem[0], dsem[1])
        key = "dma%d" % dsem[0].num
        for b in reads:
            b.reads[key] = tok
        for b in writes:
            b.writes = {key: tok}
            b.reads = {}
        return tok

    def barrier(self, engines=("pe", "act", "dve", "pool", "sp"), skip=()):
        for e in engines:
            for f in self.engs:
                if f != e and self.cnt[f] > 0:
                    self._wait(e, (self.sem[f], self.cnt[f]))
            for d in self.dsems:
                if any(d is x for x in skip):
                    continue
                self._wait(e, (d[0], d[1]))


def simulate(S_):
    sem = {}
    pc = {k: 0 for k in S_.prog}
    progress = True
    while progress:
        progress = False
        for k, prog in S_.prog.items():
            while pc[k] < len(prog):
                waits, inc, idx = prog[pc[k]]
                if all(sem.get(n, 0) >= v for n, v in waits):
                    if inc:
                        sem[inc[0]] = sem.get(inc[0], 0) + inc[1]
                    pc[k] += 1
                    progress = True
                else:
                    break
    stuck = {k: (pc[k], len(p)) for k, p in S_.prog.items() if pc[k] < len(p)}
    for k in stuck:
        waits, inc, idx = S_.prog[k][pc[k]]
        print("STUCK", k, "at", pc[k], "of", len(S_.prog[k]), "ins#", idx, "waits", [(n, v, sem.get(n, 0)) for n, v in waits])
    return not stuck


class Carver:
    def __init__(self, t, lo, hi):
        self.t, self.lo, self.hi, self.p = t, lo, hi, lo

    def reset(self):
        self.p = self.lo

    def take(self, n, dt):
        nb = n * 2 if dt == F32 else n
        self.p = (self.p + 15) // 16 * 16
        a = self.t[:, self.p:self.p + nb]
        self.p += nb
        assert self.p <= self.hi, ("carver overflow", self.p, self.hi)
        return a.bitcast(F32) if dt == F32 else a


def build_nc(S, NSEQ):
    NT = S // 128
    NQC = S // 512
    NKB = S // 128
    nc = bass.Bass("TRN2", target_bir_lowering=False)

    def din(name, shape, dt=F32):
        return nc.dram_tensor(name, shape, dt, kind="ExternalInput").ap()

    x_d = din("x", [NSEQ, S, D])
    w_in_d = din("w_in", [D, INW])
    w_uq_d = din("w_uq", [256, 768])
    w_ukv_d = din("w_ukv", [128, 1024])
    w_out_d = din("w_out", [D, D])
    w_up_d = din("w_up", [D, DFF])
    w_dn_d = din("w_down", [DFF, D])
    rows_d = din("rows", [128, NR])
    cols_d = din("cols", [128, NCOL])
    lamb_d = din("lamb", [128, 256])
    cs_d = din("cs", [S, 128])
    strips_d = din("strips", [128, 4 * STRIP_W])
    ident_d = din("ident", [128, 128], BF16)
    out_d = nc.dram_tensor("out", [NSEQ, S, D], F32, kind="ExternalOutput").ap()
    wupb = nc.dram_tensor("wupb", [16, 128, 8 * 256], BF16, kind="Internal").ap()
    wdnb = nc.dram_tensor("wdnb", [DFF, D], BF16, kind="Internal").ap()

    REG_A = 8 * S
    REG_B = max(8 * S, 32768)
    WORK_N = 24320
    with ExitStack() as st:
        S_ = Sched(nc, st)
        sb = lambda name, shape, dt: st.enter_context(nc.sbuf_tensor("sb_" + name, shape, dt))
        REG = sb("reg", [128, REG_A + REG_B], BF16)
        WORK = sb("work", [128, WORK_N], BF16)
        WPB = sb("wpb", [128, 8 * 1536], BF16)
        cols = sb("cols", [128, NCOL], F32)
        lamt = sb("lamt", [128, 384], F32)
        lams = sb("lams", [128, 16], F32)
        ident = sb("ident", [128, 128], BF16)
        ones_b = sb("ones_b", [128, 128], BF16)
        ones_f = sb("ones_f", [128, 128], F32)
        wuq = sb("wuq", [128, 2, 768], BF16)
        wukv = sb("wukv", [128, 1024], BF16)
        ps = st.enter_context(nc.psum_tensor("ps", [128, 8, 512], F32))
        bps = [Buf("ps%d" % i) for i in range(8)]

        def psb(bank):
            return ps[:, bank, :].bitcast(BF16)

        b_const = Buf("const")
        b_lam = Buf("lam")
        b_wpb = Buf("wpb")
        b_wsmall = Buf("wsmall")
        b_scr = Buf("scratch_w")

        d_const = S_.new_dma_sem("const")
        d_wsm = S_.new_dma_sem("wsm")
        d_scr = S_.new_dma_sem("scr")
        d_wp = [S_.new_dma_sem("wp0"), S_.new_dma_sem("wp1")]
        d_x = [S_.new_dma_sem("x0"), S_.new_dma_sem("x1")]
        d_cs = [S_.new_dma_sem("cs%d" % i) for i in range(4)]
        d_out = [[S_.new_dma_sem("out%d_%d" % (k, j)) for j in range(4)] for k in range(2)]
        d_strip = S_.new_dma_sem("strip")
        d_wo = S_.new_dma_sem("wo")
        d_wu = [S_.new_dma_sem("wu%d" % i) for i in range(3)]
        d_wd = [S_.new_dma_sem("wd%d" % i) for i in range(3)]
        d_xc = [[S_.new_dma_sem("xc%d_%d" % (k, j)) for j in range(4)] for k in range(2)]

        QN = REG[:, 0:4 * S].rearrange("p (h s) -> p h s", h=4)
        KN = REG[:, 4 * S:8 * S].rearrange("p (h s) -> p h s", h=4)
        DQ = KN
        o2 = REG_A
        VT = REG[:, o2:o2 + 4 * S].rearrange("p (t n) -> p t n", n=512)
        DK = REG[:, o2:o2 + 4 * S].rearrange("p (h s) -> p h s", h=4)
        QR = REG[:, o2 + 4 * S:o2 + 6 * S].rearrange("p (h s) -> p h s", h=2)
        KR = REG[:, o2 + 6 * S:o2 + 8 * S].rearrange("p (h s) -> p h s", h=2)
        DVT = REG[:, o2 + 4 * S:o2 + 8 * S].rearrange("p (t n) -> p t n", n=512)
        bQ = [[Buf("q%d_%d" % (h, c)) for c in range(NQC)] for h in range(4)]
        bK = [[Buf("k%d_%d" % (h, c)) for c in range(NQC)] for h in range(4)]
        bV = Buf("v")
        bR3 = Buf("r3")

        def all_region_bufs():
            return [b for r in bQ for b in r] + [b for r in bK for b in r] + [bV, bR3]

        S_.dma("sp", ident[:], ident_d, d_const, writes=[b_const])
        S_.dma("sp", cols[:], cols_d, d_const, writes=[b_const])
        S_.dma("sp", lamt[:, 0:256], lamb_d, d_const, writes=[b_const])
        stg_q = WORK[:, 0:3072].bitcast(F32).rearrange("p (c n) -> p c n", c=2)
        stg_kv = WORK[:, 3072:5120].bitcast(F32)
        b_stg = Buf("stg")
        S_.dma("sp", stg_q, w_uq_d.rearrange("(c p) n -> p c n", p=128), d_wsm, writes=[b_stg])
        S_.dma("sp", stg_kv, w_ukv_d, d_wsm, writes=[b_stg])
        for c in range(2):
            S_.op("dve", lambda e: e.tensor_scalar(out=wuq[:, c, :], in0=stg_q[:, c, :], scalar1=cols[:, C_QA + c:C_QA + c + 1],
                                                   scalar2=None, op0=ALU.mult), reads=[b_stg, b_const], writes=[b_wsmall])
        S_.op("dve", lambda e: e.tensor_scalar(out=wukv[:], in0=stg_kv, scalar1=cols[:, C_KVA:C_KVA + 1], scalar2=None, op0=ALU.mult),
              reads=[b_stg, b_const], writes=[b_wsmall])
        S_.op("dve", lambda e: e.memset(ones_b[:], 1.0), writes=[b_lam])
        S_.op("dve", lambda e: e.memset(ones_f[:], 1.0), writes=[b_lam])
        lv = lamt[:, 0:256].rearrange("p (a b) -> p a b", a=2)
        lprod = lamt[:, 256:384].rearrange("p (a b) -> p a b", a=2)
        S_.op("dve", lambda e: e.tensor_tensor(out=lprod, in0=lv[:, :, 0:64], in1=lv[:, :, 64:128], op=ALU.mult),
              reads=[b_const], writes=[b_const])
        S_.op("dve", lambda e: e.reduce_sum(out=lams[:, 0:2], in_=lprod, axis=AX.X), reads=[b_const], writes=[b_lam])
        S_.op("act", lambda e: e.activation(out=lams[:, 2:4], in_=lams[:, 0:2], func=AF.Exp), reads=[b_lam], writes=[b_lam])
        S_.op("dve", lambda e: e.tensor_tensor(out=lams[:, 4:5], in0=lams[:, 2:3], in1=lams[:, 3:4], op=ALU.subtract),
              reads=[b_lam], writes=[b_lam])
        S_.op("dve", lambda e: e.tensor_scalar(out=lams[:, 5:6], in0=lams[:, 4:5], scalar1=LAM_INIT, scalar2=-1.0,
                                               op0=ALU.add, op1=ALU.mult), reads=[b_lam], writes=[b_lam])
        S_.op("dve", lambda e: e.tensor_scalar(out=lams[:, 6:7], in0=cols[:, C_DOUT:C_DOUT + 1], scalar1=1.0 - LAM_INIT,
                                               scalar2=None, op0=ALU.mult), reads=[b_const, b_lam], writes=[b_lam])
        neglam = lams[:, 5:6]
        doutc = lams[:, 6:7]
        for (dst, a, b) in ((7, C_MQN, C_MKN), (8, C_MQR, C_MKR), (9, C_DQW, C_DKW)):
            S_.op("dve", lambda e: e.tensor_tensor(out=lams[:, dst:dst + 1], in0=cols[:, a:a + 1], in1=cols[:, b:b + 1], op=ALU.mult),
                  reads=[b_const, b_lam], writes=[b_lam])
        wqk_n, wqk_r, wdqk = lams[:, 7:8], lams[:, 8:9], lams[:, 9:10]
        wup_src = w_up_d.rearrange("(c p) (g n) -> g p c n", p=128, n=256)
        wup_dst = wupb.rearrange("g p (c n) -> g p c n", n=256)
        for g in range(16):
            S_.dma("pool", wup_dst[g], wup_src[g], d_scr, writes=[b_scr])
        for g in range(8):
            S_.dma("pool", wdnb[g * 512:(g + 1) * 512, :], w_dn_d[g * 512:(g + 1) * 512, :], d_scr, writes=[b_scr])

        WK = Carver(WORK, 0, WORK_N)
        S_.barrier(skip=[d_scr])

        def fold_wp(mla, lo):
            ncol = 448 if mla else 1536
            WP = WPB[:, 0:8 * ncol].rearrange("p (c n) -> p c n", c=8)
            src = w_in_d[:, 0:448] if mla else w_in_d[:, 448:INW]
            stg = [WORK[:, lo + i * 3072:lo + (i + 1) * 3072].bitcast(F32) for i in range(2)]
            bstg = [Buf("wstg0"), Buf("wstg1")]
            for c in range(8):
                S_.dma("sp", stg[c % 2][:, 0:ncol], src[c * 128:(c + 1) * 128, :], d_wp[c % 2], writes=[bstg[c % 2]])
                S_.op("dve", lambda e: e.tensor_scalar(out=WP[:, c, :], in0=stg[c % 2][:, 0:ncol], scalar1=cols[:, C_ATTN + c:C_ATTN + c + 1],
                                                       scalar2=None, op0=ALU.mult), reads=[bstg[c % 2], b_const], writes=[b_wpb])

        def stage_A(s, mla, prefolded=False):
            ncol = 448 if mla else 1536
            WP = WPB[:, 0:8 * ncol].rearrange("p (c n) -> p c n", c=8)
            if not prefolded:
                fold_wp(mla, 0)
                S_.barrier(skip=[d_scr])
            WK.reset()
            junk = WK.take(1024, BF16); bjunk = Buf("junk")
            sets = []
            NS = 4
            xts = [WK.take(1024, F32) for _ in range(2)]
            bxs = [Buf("xt0"), Buf("xt1")]
            for k in range(NS):
                d = dict(k=k)
                d["cst"] = WK.take(128, F32) if mla else None; d["bcs"] = Buf("cs%d" % k)
                d["A"] = WK.take(1024, BF16); d["bA"] = Buf("A%d" % k)
                d["B"] = WK.take(1024, BF16); d["bB"] = Buf("B%d" % k)
                d["stt"] = WK.take(64, F32); d["bst"] = Buf("st%d" % k)
                d["sq"] = WK.take(768 if mla else 1024, F32); d["bsq"] = Buf("sq%d" % k)
                d["tmp2"] = WK.take(256, F32) if mla else None; d["btmp2"] = Buf("tmp2%d" % k)
                d["krot"] = WK.take(64, F32) if mla else None; d["bkrot"] = Buf("krot%d" % k)
                sets.append(d)

            def tile_gen(t, d):
                k = d["k"]
                cst, A, B, stt, sq, tmp2, krot = (d[n] for n in ("cst", "A", "B", "stt", "sq", "tmp2", "krot"))
                bcs, bA, bB, bst, bsq, btmp2, bkrot = (d[n] for n in ("bcs", "bA", "bB", "bst", "bsq", "btmp2", "bkrot"))
                xt, bx = xts[t % 2], bxs[t % 2]
                if mla:
                    pT = pP = pY = 2 * k
                    pX = 2 * k + 1
                else:
                    pT = pP = pY = 2 * k
                    pX = 2 * k + 1
                tok = slice(t * 128, (t + 1) * 128)
                hT3 = B.rearrange("p (c t) -> p c t", c=8)

                def rstd_from_ms(dst, src_):
                    S_.op("act", lambda e: e.activation(out=dst, in_=src_, func=AF.Ln, bias=EPS), reads=[bst], writes=[bst])
                    S_.op("act", lambda e: e.activation(out=dst, in_=dst, func=AF.Exp, scale=-0.5), reads=[bst], writes=[bst])

                def rope(dst, src_, nh, rd, wr):
                    cos = cst[:, 0:64].unsqueeze(1).broadcast_to([128, nh, 64])
                    s_lo = cst[:, 64:96].unsqueeze(1).broadcast_to([128, nh, 32])
                    s_hi = cst[:, 96:128].unsqueeze(1).broadcast_to([128, nh, 32])
                    t2 = tmp2[:, 0:nh * 64].rearrange("p (h d) -> p h d", h=nh)
                    S_.op("dve", lambda e: e.tensor_tensor(out=t2[:, :, 0:32], in0=src_[:, :, 32:64], in1=s_lo, op=ALU.mult),
                          reads=rd + [bcs], writes=[btmp2])
                    S_.op("dve", lambda e: e.tensor_tensor(out=t2[:, :, 32:64], in0=src_[:, :, 0:32], in1=s_hi, op=ALU.mult),
                          reads=rd + [bcs], writes=[btmp2])
                    S_.op("dve", lambda e: e.tensor_tensor(out=dst, in0=src_, in1=cos, op=ALU.mult),
                          reads=rd + [bcs], writes=wr)
                    S_.op("dve", lambda e: e.tensor_tensor(out=dst, in0=dst, in1=t2, op=ALU.add),
                          reads=wr + [btmp2], writes=wr)

                S_.dma("sp", xt, x_d[s, tok, :], d_x[t % 2], writes=[bx])
                if mla:
                    S_.dma("sp", cst, cs_d[tok, :], d_cs[k], writes=[bcs])
                yield
                S_.op("act", lambda e: e.activation(out=junk, in_=xt, func=AF.Square, scale=1.0 / 32.0,
                                                    accum_out=stt[:, 0:1]), reads=[bx], writes=[bjunk, bst])
                rstd_from_ms(stt[:, 1:2], stt[:, 0:1])
                yield
                if F_HB:
                    S_.op("act", lambda e: e.activation(out=A, in_=xt, func=AF.Copy, scale=stt[:, 1:2]),
                          reads=[bx, bst], writes=[bA])
                else:
                    S_.op("dve", lambda e: e.tensor_scalar(out=A, in0=xt, scalar1=stt[:, 1:2], scalar2=None, op0=ALU.mult),
                          reads=[bx, bst], writes=[bA])
                yield
                for c in range(8):
                    S_.op("pe", lambda e: e.transpose(out=psb(pT)[:, c * 128:(c + 1) * 128], in_=A[:, c * 128:(c + 1) * 128],
                                                      identity=ident[:]), reads=[bA, b_const], writes=[bps[pT]], inc=(c == 7))
                yield
                if A_ACTCOPY:
                    S_.op("act", lambda e: e.activation(out=B, in_=psb(pT), func=AF.Copy), reads=[bps[pT]], writes=[bB])
                else:
                    S_.op("dve", lambda e: e.tensor_copy(out=B, in_=psb(pT)), reads=[bps[pT]], writes=[bB])
                yield
                if mla:
                    for c in range(8):
                        S_.op("pe", lambda e: e.matmul(ps[:, pP, 0:448], lhsT=hT3[:, c, :], rhs=WP[:, c, 0:448],
                                                       start=(c == 0), stop=(c == 7)),
                              reads=[bB, b_wpb], writes=[bps[pP]], inc=(c == 7))
                else:
                    for nb, bank in ((2, pY), (0, pX)):
                        for c in range(8):
                            S_.op("pe", lambda e: e.matmul(ps[:, bank, :], lhsT=hT3[:, c, :], rhs=WP[:, c, nb * 512:(nb + 1) * 512],
                                                           start=(c == 0), stop=(c == 7)),
                                  reads=[bB, b_wpb], writes=[bps[bank]], inc=(c == 7))
                yield
                if mla:
                    cn = A
                    S_.op("act", lambda e: e.activation(out=junk[:, 0:256], in_=ps[:, pP, 0:256], func=AF.Square, scale=1.0 / 16.0,
                                                        accum_out=stt[:, 2:3]), reads=[bps[pP]], writes=[bjunk, bst])
                    S_.op("act", lambda e: e.activation(out=junk[:, 0:128], in_=ps[:, pP, 256:384], func=AF.Square,
                                                        scale=128.0 ** -0.5, accum_out=stt[:, 3:4]), reads=[bps[pP]], writes=[bjunk, bst])
                    rstd_from_ms(stt[:, 4:6], stt[:, 2:4])
                    S_.op("act", lambda e: e.activation(out=junk[:, 0:64], in_=ps[:, pP, 384:448], func=AF.Square, scale=192.0 ** -0.5,
                                                        accum_out=stt[:, 24:25]), reads=[bps[pP]], writes=[bjunk, bst])
                    yield
                    if F_CN:
                        S_.op("act", lambda e: e.activation(out=cn[:, 0:256], in_=ps[:, pP, 0:256], func=AF.Copy, scale=stt[:, 4:5]),
                              reads=[bps[pP], bst], writes=[bA])
                        S_.op("act", lambda e: e.activation(out=cn[:, 256:384], in_=ps[:, pP, 256:384], func=AF.Copy, scale=stt[:, 5:6]),
                              reads=[bps[pP], bst], writes=[bA])
                    else:
                        S_.op("dve", lambda e: e.tensor_scalar(out=cn[:, 0:256], in0=ps[:, pP, 0:256], scalar1=stt[:, 4:5], scalar2=None, op0=ALU.mult),
                              reads=[bps[pP], bst], writes=[bA])
                        S_.op("dve", lambda e: e.tensor_scalar(out=cn[:, 256:384], in0=ps[:, pP, 256:384], scalar1=stt[:, 5:6], scalar2=None, op0=ALU.mult),
                              reads=[bps[pP], bst], writes=[bA])
                    kr3 = krot.rearrange("p (h d) -> p h d", h=1)
                    rope(kr3, ps[:, pP, 384:448].rearrange("p (h d) -> p h d", h=1), 1, [bps[pP]], [bkrot])
                    yield
                    for c in range(3):
                        S_.op("pe", lambda e: e.transpose(out=psb(pT)[:, c * 128:(c + 1) * 128], in_=cn[:, c * 128:(c + 1) * 128],
                                                          identity=ident[:]), reads=[bA, b_const], writes=[bps[pT]], inc=(c == 2))
                    yield
                    S_.op("dve", lambda e: e.tensor_copy(out=B[:, 0:384], in_=psb(pT)[:, 0:384]), reads=[bps[pT]], writes=[bB])
                    cT3 = B[:, 0:384].rearrange("p (c t) -> p c t", c=3)
                    yield
                    for (bank, c0, w) in ((pX, 0, 512), (pY, 512, 256)):
                        for c in range(2):
                            S_.op("pe", lambda e: e.matmul(ps[:, bank, 0:w], lhsT=cT3[:, c, :], rhs=wuq[:, c, c0:c0 + w],
                                                           start=(c == 0), stop=(c == 1)),
                                  reads=[bB, b_wsmall], writes=[bps[bank]], inc=(c == 1))
                    yield
                    S_.op("act", lambda e: e.activation(out=sq[:, 0:512], in_=ps[:, pX, :], func=AF.Square, scale=192.0 ** -0.5),
                          reads=[bps[pX]], writes=[bsq])
                    S_.op("act", lambda e: e.activation(out=sq[:, 512:768], in_=ps[:, pY, 0:256], func=AF.Square, scale=192.0 ** -0.5),
                          reads=[bps[pY]], writes=[bsq])
                    yield
                    S_.op("dve", lambda e: e.reduce_sum(out=stt[:, 8:12], in_=sq[:, 0:512].rearrange("p (h d) -> p h d", h=4), axis=AX.X),
                          reads=[bsq], writes=[bst])
                    S_.op("dve", lambda e: e.reduce_sum(out=stt[:, 12:16], in_=sq[:, 512:768].rearrange("p (h d) -> p h d", h=4), axis=AX.X),
                          reads=[bsq], writes=[bst])
                    S_.op("dve", lambda e: e.tensor_tensor(out=stt[:, 16:20], in0=stt[:, 8:12], in1=stt[:, 12:16], op=ALU.add),
                          reads=[bst], writes=[bst])
                    yield
                    rstd_from_ms(stt[:, 20:24], stt[:, 16:20])
                    yield
                    rq_n = stt[:, 20:24].unsqueeze(2).broadcast_to([128, 4, 128])
                    rq_r = stt[:, 20:24].unsqueeze(2).broadcast_to([128, 4, 64])
                    Qtok = A
                    S_.op("dve", lambda e: e.tensor_tensor(out=Qtok[:, 0:512].rearrange("p (h d) -> p h d", h=4),
                                                           in0=ps[:, pX, :].rearrange("p (h d) -> p h d", h=4), in1=rq_n, op=ALU.mult),
                          reads=[bps[pX], bst], writes=[bA])
                    qr3 = sq[:, 0:256].rearrange("p (h d) -> p h d", h=4)
                    rope(qr3, ps[:, pY, 0:256].rearrange("p (h d) -> p h d", h=4), 4, [bps[pY]], [bsq])
                    S_.op("dve", lambda e: e.tensor_tensor(out=Qtok[:, 512:768].rearrange("p (h d) -> p h d", h=4), in0=qr3, in1=rq_r,
                                                           op=ALU.mult), reads=[bsq, bst], writes=[bA])
                    yield
                    for c in range(6):
                        S_.op("pe", lambda e: e.transpose(out=psb(pT)[:, c * 128:(c + 1) * 128], in_=Qtok[:, c * 128:(c + 1) * 128],
                                                          identity=ident[:]), reads=[bA, b_const], writes=[bps[pT]], inc=(c == 5))
                    yield
                    qcb = [bQ[h][t // 4] for h in range(4)]
                    if A_ACTCOPY:
                        S_.op("act", lambda e: e.activation(out=QN[:, :, tok], in_=psb(pT)[:, 0:512].rearrange("p (h t) -> p h t", h=4),
                                                            func=AF.Copy), reads=[bps[pT]], writes=qcb)
                        S_.op("act", lambda e: e.activation(out=QR[:, :, tok], in_=psb(pT)[:, 512:768].rearrange("p (h t) -> p h t", h=2),
                                                            func=AF.Copy), reads=[bps[pT]], writes=[bR3])
                    else:
                        S_.op("dve", lambda e: e.tensor_copy(out=QN[:, :, tok], in_=psb(pT)[:, 0:512].rearrange("p (h t) -> p h t", h=4)),
                              reads=[bps[pT]], writes=qcb)
                        S_.op("dve", lambda e: e.tensor_copy(out=QR[:, :, tok], in_=psb(pT)[:, 512:768].rearrange("p (h t) -> p h t", h=2)),
                              reads=[bps[pT]], writes=[bR3])
                    yield
                    for (bank, c0) in ((pX, 0), (pY, 512)):
                        S_.op("pe", lambda e: e.matmul(ps[:, bank, :], lhsT=cT3[:, 2, :], rhs=wukv[:, c0:c0 + 512], start=True, stop=True),
                              reads=[bB, b_wsmall], writes=[bps[bank]])
                    yield
                    S_.op("act", lambda e: e.activation(out=sq[:, 0:512], in_=ps[:, pX, :], func=AF.Square, scale=192.0 ** -0.5),
                          reads=[bps[pX]], writes=[bsq])
                    S_.op("act", lambda e: e.activation(out=VT[:, t, :], in_=ps[:, pY, :], func=AF.Copy), reads=[bps[pY]], writes=[bV])
                    yield
                    S_.op("dve", lambda e: e.reduce_sum(out=stt[:, 8:12], in_=sq[:, 0:512].rearrange("p (h d) -> p h d", h=4), axis=AX.X),
                          reads=[bsq], writes=[bst])
                    S_.op("dve", lambda e: e.tensor_scalar(out=stt[:, 16:20], in0=stt[:, 8:12], scalar1=stt[:, 24:25], scalar2=None,
                                                           op0=ALU.add), reads=[bst], writes=[bst])
                    yield
                    rstd_from_ms(stt[:, 28:32], stt[:, 16:20])
                    yield
                    rk_n = stt[:, 28:32].unsqueeze(2).broadcast_to([128, 4, 128])
                    rk_r = stt[:, 28:32].unsqueeze(2).broadcast_to([128, 4, 64])
                    Ktok = A
                    S_.op("dve", lambda e: e.tensor_tensor(out=Ktok[:, 0:512].rearrange("p (h d) -> p h d", h=4),
                                                           in0=ps[:, pX, :].rearrange("p (h d) -> p h d", h=4), in1=rk_n, op=ALU.mult),
                          reads=[bps[pX], bst], writes=[bA])
                    S_.op("dve", lambda e: e.tensor_tensor(out=Ktok[:, 512:768].rearrange("p (h d) -> p h d", h=4),
                                                           in0=krot.unsqueeze(1).broadcast_to([128, 4, 64]), in1=rk_r, op=ALU.mult),
                          reads=[bkrot, bst], writes=[bA])
                    yield
                    for c in range(6):
                        S_.op("pe", lambda e: e.transpose(out=psb(pT)[:, c * 128:(c + 1) * 128], in_=Ktok[:, c * 128:(c + 1) * 128],
                                                          identity=ident[:]), reads=[bA, b_const], writes=[bps[pT]], inc=(c == 5))
                    yield
                    kcb = [bK[h][t // 4] for h in range(4)]
                    S_.op("dve", lambda e: e.tensor_scalar(out=KN[:, :, tok], in0=psb(pT)[:, 0:512].rearrange("p (h t) -> p h t", h=4),
                                                           scalar1=wqk_n, scalar2=None, op0=ALU.mult),
                          reads=[bps[pT], b_lam], writes=kcb)
                    S_.op("dve", lambda e: e.tensor_scalar(out=KR[:, :, tok], in0=psb(pT)[:, 512:768].rearrange("p (h t) -> p h t", h=2),
                                                           scalar1=wqk_r, scalar2=None, op0=ALU.mult),
                          reads=[bps[pT], b_lam], writes=[bR3])
                    yield
                else:
                    cn = A
                    S_.op("act", lambda e: e.activation(out=DVT[:, t, :], in_=ps[:, pY, :], func=AF.Copy), reads=[bps[pY]], writes=[bR3])
                    for (which, s0, r0, dst) in ((0, 8, 32, 0), (1, 16, 40, 512)):
                        if which == 1:
                            for c in range(8):
                                S_.op("pe", lambda e: e.matmul(ps[:, pX, :], lhsT=hT3[:, c, :], rhs=WP[:, c, 512:1024],
                                                               start=(c == 0), stop=(c == 7)),
                                      reads=[bB, b_wpb], writes=[bps[pX]], inc=(c == 7))
                            yield
                        sqh = sq[:, dst:dst + 512]
                        S_.op("act", lambda e: e.activation(out=sqh, in_=ps[:, pX, :], func=AF.Square, scale=0.125),
                              reads=[bps[pX]], writes=[bsq])
                        yield
                        S_.op("dve", lambda e: e.reduce_sum(out=stt[:, s0:s0 + 8], in_=sqh.rearrange("p (h d) -> p h d", h=8), axis=AX.X),
                              reads=[bsq], writes=[bst])
                        yield
                        rstd_from_ms(stt[:, r0:r0 + 8], stt[:, s0:s0 + 8])
                        yield
                        S_.op("dve", lambda e: e.tensor_tensor(out=cn[:, dst:dst + 512].rearrange("p (h d) -> p h d", h=8),
                                                               in0=ps[:, pX, :].rearrange("p (h d) -> p h d", h=8),
                                                               in1=stt[:, r0:r0 + 8].unsqueeze(2).broadcast_to([128, 8, 64]), op=ALU.mult),
                              reads=[bps[pX], bst], writes=[bA])
                        yield
                    for c in range(8):
                        S_.op("pe", lambda e: e.transpose(out=psb(pT)[:, c * 128:(c + 1) * 128], in_=cn[:, c * 128:(c + 1) * 128],
                                                          identity=ident[:]), reads=[bA, b_const], writes=[bps[pT]], inc=(c == 7))
                    yield
                    kcb = [bK[h][t // 4] for h in range(4)]
                    S_.op("dve", lambda e: e.tensor_copy(out=DQ[:, :, tok], in_=psb(pT)[:, 0:512].rearrange("p (h t) -> p h t", h=4)),
                          reads=[bps[pT]], writes=kcb)
                    if A_ACTCOPY:
                        S_.op("act", lambda e: e.activation(out=DK[:, :, tok], in_=psb(pT)[:, 512:1024].rearrange("p (h t) -> p h t", h=4),
                                                            func=AF.Copy), reads=[bps[pT]], writes=[bV])
                    else:
                        S_.op("dve", lambda e: e.tensor_scalar(out=DK[:, :, tok], in0=psb(pT)[:, 512:1024].rearrange("p (h t) -> p h t", h=4),
                                                               scalar1=wdqk, scalar2=None, op0=ALU.mult),
                              reads=[bps[pT], b_lam], writes=[bV])
                    yield

            active = []
            nxt = 0
            STAG = 5 if mla else 4
            while nxt < NT or active:
                if nxt < NT and (not active or (A_INTERLEAVE and len(active) < NS and active[-1][1] >= STAG)):
                    d = sets[nxt % NS]
                    active.append([tile_gen(nxt, d), 0])
                    nxt += 1
                for a in list(active):
                    try:
                        next(a[0])
                        a[1] += 1
                    except StopIteration:
                        active.remove(a)

        def stage_B(mla):
            WK.reset()
            PT = [WK.take(1024, BF16) for _ in range(3)]
            bpt = [Buf("pt%d" % i) for i in range(3)]
            QM = [WK.take(512, BF16) for _ in range(2)]
            bqm = [Buf("qm0"), Buf("qm1")]
            tr = WK.take(512, F32); btr = Buf("tr")
            to = WK.take(512, F32); bto = Buf("to")
            tt = WK.take(512, F32); btt = Buf("tt")
            tsq, btsq, trs, btrs = tr, btr, tt, btt
            bstf = Buf("stf")
            if not mla:
                stf = WK.take(4 * STRIP_W, F32)
                BH = WPB[:, 0:4 * STRIP_W].rearrange("p (h w) -> p h w", h=4)
                BL = WPB[:, 4 * STRIP_W:8 * STRIP_W].rearrange("p (h w) -> p h w", h=4)
                S_.dma("sp", stf, strips_d, d_strip, writes=[bstf])
                S_.op("dve", lambda e: e.tensor_scalar(out=stf, in0=stf, scalar1=8.0, scalar2=None, op0=ALU.mult),
                      reads=[bstf], writes=[bstf])
                S_.op("dve", lambda e: e.tensor_copy(out=WPB[:, 0:4 * STRIP_W], in_=stf), reads=[bstf], writes=[b_wpb])
                S_.op("dve", lambda e: e.tensor_tensor(out=WPB[:, 4 * STRIP_W:8 * STRIP_W], in0=stf, in1=WPB[:, 0:4 * STRIP_W],
                                                       op=ALU.subtract), reads=[bstf, b_wpb], writes=[b_wpb])
            bacc = [Buf("acc0"), Buf("acc1")]
            units = []
            if mla:
                for h in range(4):
                    for qc in range(NQC):
                        units.append(dict(h=h, qc=qc, m=0))
            else:
                for h in range(4):
                    for qc in range(NQC):
                        for m in range(2):
                            units.append(dict(h=h, qc=qc, m=m))
            NSL = NKB // 2
            slots = [(ui, j) for ui in range(len(units)) for j in range(NSL)]
            deferred = []
            L = 1

            def prep_qm(ui):
                u = units[ui]
                h, qc, m = u["h"], u["qc"], u["m"]
                qs = slice(qc * 512, (qc + 1) * 512)
                qm = QM[ui % 2]
                if mla:
                    p0 = (h % 2) * 64
                    srcq = QR[p0:p0 + 64, h // 2, qs]
                    rb_ = [bR3]
                else:
                    p0 = m * 64
                    srcq = DQ[p0:p0 + 64, h, qs]
                    rb_ = [bK[h][qc]]
                pd = 64 - p0
                S_.op("pool", lambda e: e.memset(qm[pd:pd + 64, :], 0.0), writes=[bqm[ui % 2]])
                S_.op("pool", lambda e: e.tensor_copy(out=qm[p0:p0 + 64, :], in_=srcq), reads=rb_, writes=[bqm[ui % 2]])

            def bias_kind(qc, kb):
                if mla:
                    return "none"
                if kb < 4 * qc - 1:
                    return "neg"
                if kb > 4 * qc + 4:
                    return "pos"
                return "near"

            def emit_scores(g):
                ui, j = slots[g]
                u = units[ui]
                h, qc, m = u["h"], u["qc"], u["m"]
                qs = slice(qc * 512, (qc + 1) * 512)
                r = g % 2
                if g == 0:
                    prep_qm(0)
                if j == NSL // 2 and ui + 1 < len(units):
                    prep_qm(ui + 1)
                qm = QM[ui % 2]
                for t_ in range(2):
                    kb = 2 * j + t_
                    near = bias_kind(qc, kb) == "near"
                    ks = slice(kb * 128, (kb + 1) * 128)
                    bank = 2 * r + t_
                    last = (t_ == 1)
                    if mla:
                        S_.op("pe", lambda e: e.matmul(ps[:, bank, :], lhsT=KN[:, h, ks], rhs=QN[:, h, qs], start=True, stop=False),
                              reads=[bK[h][kb // 4], bQ[h][qc]], writes=[bps[bank]], inc=False)
                        S_.op("pe", lambda e: e.matmul(ps[:, bank, :], lhsT=KR[:, h // 2, ks], rhs=qm, start=False, stop=True),
                              reads=[bR3, bqm[ui % 2]], writes=[bps[bank]], inc=last)
                    else:
                        S_.op("pe", lambda e: e.matmul(ps[:, bank, :], lhsT=DK[:, h, ks], rhs=qm, start=True, stop=(not near)),
                              reads=[bV, bqm[ui % 2]], writes=[bps[bank]], inc=(not near))
                        if near:
                            off = DMAX - (kb - 4 * qc) * 128
                            S_.op("pe", lambda e: e.matmul(ps[:, bank, :], lhsT=ident[:], rhs=BH[:, h, off:off + 512], start=False, stop=False),
                                  reads=[b_wpb, b_const], writes=[bps[bank]], inc=False)
                            S_.op("pe", lambda e: e.matmul(ps[:, bank, :], lhsT=ident[:], rhs=BL[:, h, off:off + 512], start=False, stop=True),
                                  reads=[b_wpb, b_const], writes=[bps[bank]], inc=True)

            def emit_exp_pv(g):
                ui, j = slots[g]
                u = units[ui]
                h, qc, m = u["h"], u["qc"], u["m"]
                r = g % 2
                pt = PT[g % 3]
                bp = bpt[g % 3]
                ob, sbk = (4, 5) if ui % 2 == 0 else (6, 7)
                if mla:
                    scale = 192.0 ** -0.5
                    bias = 0.0
                    vb = bV
                else:
                    scale = 0.125
                    vb = bR3

                def bias_of(kind):
                    if kind == "neg":
                        return cols[:, C_NEG + h:C_NEG + h + 1]
                    if kind == "pos":
                        return cols[:, C_POS + h:C_POS + h + 1]
                    return 0.0
                k0, k1 = bias_kind(qc, 2 * j), bias_kind(qc, 2 * j + 1)
                if k0 == k1:
                    S_.op("act", lambda e: e.activation(out=pt.rearrange("p (b n) -> p b n", b=2), in_=ps[:, 2 * r:2 * r + 2, :],
                                                        func=AF.Exp, scale=scale, bias=bias_of(k0)),
                          reads=[bps[2 * r], bps[2 * r + 1], b_const], writes=[bp])
                else:
                    for t_, kk in ((0, k0), (1, k1)):
                        S_.op("act", lambda e: e.activation(out=pt[:, t_ * 512:(t_ + 1) * 512], in_=ps[:, 2 * r + t_, :],
                                                            func=AF.Exp, scale=scale, bias=bias_of(kk)),
                              reads=[bps[2 * r + t_], b_const], writes=[bp])
                for t_ in range(2):
                    kb = 2 * j + t_
                    vap = (VT if mla else DVT)[:, kb, h * 128:(h + 1) * 128]
                    S_.op("pe", lambda e: e.matmul(ps[:, ob, :], lhsT=vap, rhs=pt[:, t_ * 512:(t_ + 1) * 512],
                                                   start=(kb == 0), stop=(kb == NKB - 1)),
                          reads=[vb, bp], writes=[bacc[ui % 2]], inc=False)
                for t_ in range(2):
                    kb = 2 * j + t_
                    S_.op("pe", lambda e: e.matmul(ps[:, sbk, :], lhsT=ones_b[:], rhs=pt[:, t_ * 512:(t_ + 1) * 512],
                                                   start=(kb == 0), stop=(kb == NKB - 1)),
                          reads=[b_lam, bp], writes=[bacc[ui % 2]], inc=(t_ == 1))
                if j == NSL - 1:
                    finalize(ui, g)

            def finalize(ui, g):
                u = units[ui]
                h, qc, m = u["h"], u["qc"], u["m"]
                ob, sbk = (4, 5) if ui % 2 == 0 else (6, 7)
                ba = bacc[ui % 2]
                qs = slice(qc * 512, (qc + 1) * 512)
                S_.op("dve", lambda e: e.reciprocal(out=tr, in_=ps[:, sbk, :]), reads=[ba], writes=[btr])
                if mla:
                    S_.op("dve", lambda e: e.tensor_tensor(out=QN[:, h, qs], in0=ps[:, ob, :], in1=tr, op=ALU.mult),
                          reads=[ba, btr], writes=[bQ[h][qc]])
                    return
                if m == 0:
                    S_.op("dve", lambda e: e.tensor_tensor(out=to, in0=ps[:, ob, :], in1=tr, op=ALU.mult),
                          reads=[ba, btr], writes=[bto])
                    return
                S_.op("dve", lambda e: e.tensor_tensor(out=tt, in0=ps[:, ob, :], in1=tr, op=ALU.mult),
                      reads=[ba, btr], writes=[btt])
                S_.op("dve", lambda e: e.scalar_tensor_tensor(out=to, in0=tt, scalar=neglam, in1=to, op0=ALU.mult, op1=ALU.add),
                      reads=[btt, bto, b_lam], writes=[bto])
                S_.op("dve", lambda e: e.tensor_tensor(out=tsq, in0=to, in1=to, op=ALU.mult), reads=[bto], writes=[btsq])

                def t1():
                    S_.op("pe", lambda e: e.matmul(ps[:, sbk, :], lhsT=ones_f[:], rhs=tsq, start=True, stop=True),
                          reads=[btsq, b_lam], writes=[ba])

                def t2():
                    S_.op("act", lambda e: e.activation(out=trs, in_=ps[:, sbk, :], func=AF.Ln, scale=1.0 / 128.0, bias=EPS),
                          reads=[ba], writes=[btrs])

                def t3():
                    S_.op("act", lambda e: e.activation(out=trs, in_=trs, func=AF.Exp, scale=-0.5), reads=[btrs], writes=[btrs])
                    S_.op("dve", lambda e: e.scalar_tensor_tensor(out=DQ[:, h, qs], in0=to, scalar=doutc, in1=trs,
                                                                  op0=ALU.mult, op1=ALU.mult),
                          reads=[bto, btrs, b_lam], writes=[bK[h][qc]])
                o3 = min(8, NSL - 1)
                o2 = max(1, o3 - 2)
                o1 = max(1, o3 - 3)
                deferred.append((g + o1, t1))
                deferred.append((g + o2, t2))
                deferred.append((g + o3, t3))

            NG = len(slots)
            for g in range(NG + L):
                if g < NG:
                    emit_scores(g)
                if g >= L:
                    emit_exp_pv(g - L)
                while deferred and deferred[0][0] <= g:
                    deferred.pop(0)[1]()
            while deferred:
                deferred.pop(0)[1]()

        def stage_C(s):
            WK.reset()
            x1s = [[WK.take(1024, F32) for _ in range(4)], None]
            bx1s = [[Buf("x1_%d_%d" % (k, j)) for j in range(4)] for k in range(2)]
            hb2 = [WK.take(1024, BF16) for _ in range(2)]
            bhb2 = [Buf("hb2_0"), Buf("hb2_1")]
            h2T = WK.take(8 * 512, BF16); bh2T = Buf("h2T")
            h2T3 = h2T.rearrange("p (c t) -> p c t", c=8)
            junk = WK.take(1024, BF16); bjunk = Buf("junkc")
            rows = WK.take(1024, F32); b_rows = Buf("rows")
            S_.dma("sp", rows, rows_d, d_strip, writes=[b_rows])
            stt = WK.take(16, F32); bst = [Buf("stc%d" % j) for j in range(4)]
            rl = [WK.take(512, F32) for _ in range(2)]
            brl = [Buf("rl0"), Buf("rl1")]
            o2_ = REG_A
            wout = REG[:, o2_:o2_ + 8192].rearrange("p (c n) -> p c n", c=8); bwout = Buf("wout")
            aT = REG[:, o2_ + 8192:o2_ + 8192 + 16384].rearrange("p (f t) -> p f t", f=32); baT = Buf("aT")
            x1s[1] = [REG[:, o2_ + 24576 + j * 2048:o2_ + 24576 + (j + 1) * 2048].bitcast(F32) for j in range(4)]
            wu = [WPB[:, i * 2048:(i + 1) * 2048].rearrange("p (c n) -> p c n", c=8) for i in range(3)]
            wd = [WPB[:, 6144 + i * 2048:6144 + (i + 1) * 2048].rearrange("p (f n) -> p f n", f=2) for i in range(3)]
            bwu = [Buf("wu%d" % i) for i in range(3)]
            bwd = [Buf("wd%d" % i) for i in range(3)]
            S_.dma("pool", wout, w_out_d.rearrange("(c p) n -> p c n", p=128), d_wo, writes=[bwout])
            wupv = wupb.rearrange("g p (c n) -> g p c n", n=256)
            wdnv = wdnb.rearrange("(g f p) n -> g p f n", f=2, p=128)

            def load_x(ck):
                k = ck % 2
                for j in range(4):
                    tok = slice(ck * 512 + j * 128, ck * 512 + (j + 1) * 128)
                    S_.dma("sp", x1s[k][j], x_d[s, tok, :], d_xc[k][j], writes=[bx1s[k][j]])

            load_x(0)
            gi = 0
            for ck in range(NQC):
                k = ck % 2
                x1, bx1 = x1s[k], bx1s[k]
                rdb = [bQ[h][ck] for h in range(4)] + [bK[h][ck] for h in range(4)] + [bwout]
                for j in range(4):
                    tq = slice(ck * 512 + j * 128, ck * 512 + (j + 1) * 128)
                    for half in range(2):
                        bank = 2 * j + half
                        for c in range(8):
                            lhs = QN[:, c, tq] if c < 4 else DQ[:, c - 4, tq]
                            S_.op("pe", lambda e: e.matmul(ps[:, bank, :], lhsT=lhs, rhs=wout[:, c, half * 512:(half + 1) * 512],
                                                           start=(c == 0), stop=(c == 7)),
                                  reads=rdb, writes=[bps[bank]], inc=(c == 7))
                if ck + 1 < NQC:
                    load_x(ck + 1)
                def p2(j):
                    S_.op("dve", lambda e: e.tensor_tensor(out=x1[j], in0=x1[j], in1=ps[:, 2 * j:2 * j + 2, :].rearrange("p b n -> p (b n)"),
                                                           op=ALU.add), reads=[bx1[j], bps[2 * j], bps[2 * j + 1]], writes=[bx1[j]])
                    S_.op("act", lambda e: e.activation(out=junk, in_=x1[j], func=AF.Square, scale=1.0 / 32.0,
                                                        accum_out=stt[:, 2 * j:2 * j + 1]), reads=[bx1[j]], writes=[bjunk, bst[j]])
                    S_.op("act", lambda e: e.activation(out=stt[:, 2 * j + 1:2 * j + 2], in_=stt[:, 2 * j:2 * j + 1], func=AF.Ln, bias=EPS),
                          reads=[bst[j]], writes=[bst[j]])
                    S_.op("act", lambda e: e.activation(out=stt[:, 2 * j + 1:2 * j + 2], in_=stt[:, 2 * j + 1:2 * j + 2], func=AF.Exp, scale=-0.5),
                          reads=[bst[j]], writes=[bst[j]])
                    S_.op("dve", lambda e: e.scalar_tensor_tensor(out=hb2[j % 2], in0=x1[j], scalar=stt[:, 2 * j + 1:2 * j + 2],
                                                                  in1=rows[:, R_MLP:R_MLP + 1024], op0=ALU.mult, op1=ALU.mult),
                          reads=[bx1[j], bst[j], b_rows], writes=[bhb2[j % 2]])

                def p3(j):
                    bank = 2 * j
                    for c in range(8):
                        S_.op("pe", lambda e: e.transpose(out=psb(bank)[:, c * 128:(c + 1) * 128], in_=hb2[j % 2][:, c * 128:(c + 1) * 128],
                                                          identity=ident[:]), reads=[bhb2[j % 2], b_const], writes=[bps[bank]], inc=(c == 7))
                    S_.op("dve", lambda e: e.tensor_copy(out=h2T3[:, :, j * 128:(j + 1) * 128],
                                                         in_=psb(bank).rearrange("p (c t) -> p c t", c=8)),
                          reads=[bps[bank]], writes=[bh2T])
                p2(0); p2(1); p3(0); p2(2); p3(1); p2(3); p3(2); p3(3)
                for g in range(16):
                    r = (gi + g) % 3
                    S_.dma("pool", wu[r], wupv[g], d_wu[r], reads=[b_scr], writes=[bwu[r]])
                    for fl in range(2):
                        f = g * 2 + fl
                        bank = 1 + 2 * (f % 2)
                        for c in range(8):
                            S_.op("pe", lambda e: e.matmul(ps[:, bank, :], lhsT=wu[r][:, c, fl * 128:(fl + 1) * 128], rhs=h2T3[:, c, :],
                                                           start=(c == 0), stop=(c == 7)),
                                  reads=[bwu[r], bh2T], writes=[bps[bank]], inc=(c == 7))
                        S_.op("act", lambda e: e.activation(out=rl[f % 2], in_=ps[:, bank, :], func=AF.Relu),
                              reads=[bps[bank]], writes=[brl[f % 2]])
                        S_.op("dve", lambda e: e.tensor_tensor(out=aT[:, f, :], in0=rl[f % 2], in1=rl[f % 2], op=ALU.mult),
                              reads=[brl[f % 2]], writes=[baT])
                for g in range(16):
                    r = (gi + g) % 3
                    S_.dma("pool", wd[r], wdnv[g], d_wd[r], reads=[b_scr], writes=[bwd[r]])
                    for fl in range(2):
                        f = g * 2 + fl
                        for j in range(4):
                            for half in range(2):
                                bank = j * 2 + half
                                S_.op("pe", lambda e: e.matmul(ps[:, bank, :], lhsT=aT[:, f, j * 128:(j + 1) * 128],
                                                               rhs=wd[r][:, fl, half * 512:(half + 1) * 512],
                                                               start=(f == 0), stop=(f == 31)),
                                      reads=[baT, bwd[r]], writes=[bps[bank]], inc=(f == 31 or (j == 3 and half == 1)))
                gi += 16
                for j in range(4):
                    tok = slice(ck * 512 + j * 128, ck * 512 + (j + 1) * 128)
                    S_.op("dve", lambda e: e.tensor_tensor(out=x1[j], in0=x1[j],
                                                           in1=ps[:, 2 * j:2 * j + 2, :].rearrange("p b n -> p (b n)"), op=ALU.add),
                          reads=[bx1[j], bps[2 * j], bps[2 * j + 1]], writes=[bx1[j]])
                    S_.dma("sp", out_d[s, tok, :], x1[j], d_out[k][j], reads=[bx1[j]])

        for s in range(NSEQ):
            stage_A(s, True)
            S_.barrier()
            fold_wp(False, WORK_N - 6144)
            stage_B(True)
            S_.barrier()
            stage_A(s, False, prefolded=True)
            S_.barrier()
            stage_B(False)
            S_.barrier()
            stage_C(s)
            S_.barrier()
        S_.barrier(("sp",))
        print("build: n_ins", S_.n_ins, "n_wait", S_.n_wait, "nsem", S_.nsem)
        if DEBUG_SIM:
            print("simulate ok:", simulate(S_))
    return nc


def _t5_bucket_np(rel):
    half = 16
    ret = np.where(rel > 0, half, 0)
    n = np.abs(rel)
    max_exact = half // 2
    large = max_exact + (np.log(np.maximum(n, 1).astype(np.float32) / np.float32(max_exact))
                         / np.float32(math.log(128 / max_exact)) * np.float32(half - max_exact)).astype(np.int32)
    large = np.minimum(large, half - 1)
    return ret + np.where(n < max_exact, n, large)


def _host_tables(S):
    inv = (np.float32(10000.0) ** (-np.arange(0, 64, 2, dtype=np.float32) / np.float32(64))).astype(np.float32)
    ang = np.arange(S, dtype=np.float32)[:, None] * inv[None, :]
    ang = np.concatenate([ang, ang], axis=-1)
    cos = np.cos(ang).astype(np.float32)
    sin = np.sin(ang).astype(np.float32)
    sinS = sin.copy()
    sinS[:, :32] = -sinS[:, :32]
    cs = np.concatenate([cos, sinS], axis=1).astype(np.float32)
    k = np.arange(128)[:, None]
    j = np.arange(STRIP_W)[None, :]
    bucket = _t5_bucket_np((k - j + DMAX).astype(np.int32))
    return cs, bucket


def prepare_inputs(S, x_core_list, p):
    cs, bucket = _host_tables(S)
    f32 = np.float32
    w_uq = np.asarray(p["w_uq"][0], f32)
    perm_q = np.concatenate([np.arange(h * 192, h * 192 + 128) for h in range(4)]
                            + [np.arange(h * 192 + 128, h * 192 + 192) for h in range(4)])
    w_ukv = np.asarray(p["w_ukv"][0], f32)
    perm_kv = np.concatenate([np.arange(h * 256, h * 256 + 128) for h in range(4)]
                             + [np.arange(h * 256 + 128, h * 256 + 256) for h in range(4)])
    row = np.asarray(p["mlp_norm_w"][0], f32)
    assert row.shape[0] == NR
    rows = np.ascontiguousarray(np.broadcast_to(row[None, :], (128, NR)))
    rb = np.asarray(p["rel_bias"], f32)
    cols = np.zeros((128, NCOL), f32)
    cols[:, C_NEG:C_NEG + 4] = rb[15][None, :]
    cols[:, C_POS:C_POS + 4] = rb[31][None, :]
    cols[:, C_DOUT] = np.asarray(p["diff_out_norm_w"][0], f32)
    cols[:, C_ATTN:C_ATTN + 8] = np.asarray(p["attn_norm_w"][0], f32).reshape(8, 128).T
    cols[:, C_QA:C_QA + 2] = np.asarray(p["q_a_norm_w"][0], f32).reshape(2, 128).T
    cols[:, C_KVA] = np.asarray(p["kv_a_norm_w"][0], f32)
    mq = np.asarray(p["mla_q_norm_w"][0], f32); mk = np.asarray(p["mla_k_norm_w"][0], f32)
    cols[:, C_MQN] = mq[0:128]; cols[:, C_MQR] = np.tile(mq[128:192], 2)
    cols[:, C_MKN] = mk[0:128]; cols[:, C_MKR] = np.tile(mk[128:192], 2)
    cols[:, C_DQW] = np.tile(np.asarray(p["diff_q_norm_w"][0], f32), 2)
    cols[:, C_DKW] = np.tile(np.asarray(p["diff_k_norm_w"][0], f32), 2)
    lam = np.concatenate([np.asarray(p[k][0], f32) for k in ("lambda_q1", "lambda_k1", "lambda_q2", "lambda_k2")])
    lamb = np.ascontiguousarray(np.broadcast_to(lam[None, :], (128, 256)))
    strips = np.ascontiguousarray(np.transpose(rb[bucket], (0, 2, 1)).reshape(128, 4 * STRIP_W))
    shared = {
        "w_in": np.ascontiguousarray(p["w_in"][0], dtype=f32),
        "w_uq": np.ascontiguousarray(w_uq[:, perm_q]),
        "w_ukv": np.ascontiguousarray(w_ukv[:, perm_kv]),
        "w_out": np.ascontiguousarray(p["w_out"][0], dtype=f32),
        "w_up": np.ascontiguousarray(p["w_up"][0], dtype=f32),
        "w_down": np.ascontiguousarray(p["w_down"][0], dtype=f32),
        "rows": rows, "cols": cols, "lamb": lamb, "cs": cs, "strips": strips,
        "ident": np.eye(128).astype(ml_dtypes.bfloat16),
    }
    return [dict(shared, x=np.ascontiguousarray(xc, dtype=f32)) for xc in x_core_list]


def kernel(**inputs):
    x = np.asarray(inputs["x"], np.float32)
    B, S, _ = x.shape
    n = 8
    nseq = B // n
    p = {k: np.asarray(v) for k, v in inputs.items() if k != "x"}
    in_maps = prepare_inputs(S, [x[i * nseq:(i + 1) * nseq] for i in range(n)], p)
    nc = build_nc(S, nseq)
    res = run_bass_kernel_spmd(nc, in_maps, core_ids=list(range(n)))
    return np.concatenate([np.asarray(r["out"], np.float32) for r in res.results], axis=0)
```

```python
import math
from contextlib import ExitStack

import numpy as np
import ml_dtypes

import concourse.bass as bass
import concourse.mybir as mybir
from concourse.bass_utils import run_bass_kernel_spmd

F32 = mybir.dt.float32
BF16 = mybir.dt.bfloat16
AF = mybir.ActivationFunctionType
ALU = mybir.AluOpType
AX = mybir.AxisListType

D = 1024
INW = 1984
DFF = 4096
EPS = 1e-6
LAM_INIT = 0.8 - 0.6 * math.exp(-0.3 * 0)
SEM_ROLL = 8000
DEBUG_SIM = False
STRICT_SAME_ENGINE = False
A_INTERLEAVE = True
A_ACTCOPY = False
import os
F_HB = os.environ.get('F_HB', '1') == '1'
F_CN = os.environ.get('F_CN', '0') == '1'
F_KS = os.environ.get('F_KS', '1') == '1'
STRIP_W = 1408
DMAX = 640

R_MLP, NR = 0, 1024
C_NEG, C_POS, C_DOUT, C_ATTN, C_QA, C_KVA, C_MQN, C_MQR, C_MKN, C_MKR, C_DQW, C_DKW, NCOL = 0, 4, 8, 9, 17, 19, 20, 21, 22, 23, 24, 25, 32


class Buf:
    __slots__ = ("name", "writes", "reads")

    def __init__(self, name):
        self.name = name
        self.writes = {}
        self.reads = {}


class Sched:
    def __init__(self, nc, stack):
        self.nc = nc
        self.stack = stack
        self.engs = {"pe": nc.tensor, "act": nc.scalar, "dve": nc.vector,
                     "pool": nc.gpsimd, "sp": nc.sync}
        self.sem = {}
        self.cnt = {}
        self.nsem = 0
        self.dsems = []
        for k in self.engs:
            self._new_sem(k)
        self.waited = {}
        self.n_wait = 0
        self.n_ins = 0
        self.prog = {k: [] for k in self.engs}
        self.pend = {k: [] for k in self.engs}

    def _new_sem(self, k):
        self.nsem += 1
        self.sem[k] = self.stack.enter_context(self.nc.semaphore(f"s{self.nsem}_{k}"))
        self.cnt[k] = 0

    def new_dma_sem(self, name):
        self.nsem += 1
        d = [self.stack.enter_context(self.nc.semaphore(f"d{self.nsem}_{name}")), 0]
        self.dsems.append(d)
        return d

    def _wait(self, eng, tok):
        sem, val = tok
        if val <= 0:
            return
        key = (eng, sem.num)
        if self.waited.get(key, 0) >= val:
            return
        self.waited[key] = val
        self.engs[eng].wait_ge(sem, val)
        self.pend[eng].append((sem.num, val))
        self.n_wait += 1

    def _hazards(self, eng, reads, writes):
        for b in reads:
            for k, t in b.writes.items():
                if k == eng and eng in ("pe", "sp"):
                    continue
                self._wait(eng, t)
        for b in writes:
            for k, t in b.reads.items():
                if k != eng or (STRICT_SAME_ENGINE and eng not in ("pe", "sp")):
                    self._wait(eng, t)
            for k, t in b.writes.items():
                if k != eng or (STRICT_SAME_ENGINE and eng not in ("pe", "sp")):
                    self._wait(eng, t)

    def op(self, eng, fn, reads=(), writes=(), inc=True):
        self._hazards(eng, reads, writes)
        ins = fn(self.engs[eng])
        self.n_ins += 1
        if self.cnt[eng] >= SEM_ROLL:
            self._new_sem(eng)
        self.prog[eng].append((self.pend[eng], (self.sem[eng].num, 1) if inc else None, self.n_ins))
        self.pend[eng] = []
        if inc:
            self.cnt[eng] += 1
            ins.then_inc(self.sem[eng], 1)
            tok = (self.sem[eng], self.cnt[eng])
        else:
            tok = (self.sem[eng], self.cnt[eng] + 1)
        for b in reads:
            b.reads[eng] = tok
        for b in writes:
            b.writes = {eng: tok}
            b.reads = {}
        return tok

    def dma(self, q, out, in_, dsem, reads=(), writes=(), **kw):
        self._hazards(q, reads, writes)
        ins = self.engs[q].dma_start(out=out, in_=in_, **kw)
        self.n_ins += 1
        self.prog[q].append((self.pend[q], (dsem[0].num, 16), self.n_ins))
        self.pend[q] = []
        dsem[1] += 16
        ins.then_inc(dsem[0], 16)
        tok = (dsem[0], dsem[1])
        key = "dma%d" % dsem[0].num
        for b in reads:
            b.reads[key] = tok
        for b in writes:
            b.writes = {key: tok}
            b.reads = {}
        return tok

    def barrier(self, engines=("pe", "act", "dve", "pool", "sp"), skip=()):
        for e in engines:
            for f in self.engs:
                if f != e and self.cnt[f] > 0:
                    self._wait(e, (self.sem[f], self.cnt[f]))
            for d in self.dsems:
                if any(d is x for x in skip):
                    continue
                self._wait(e, (d[0], d[1]))


def simulate(S_):
    sem = {}
    pc = {k: 0 for k in S_.prog}
    progress = True
    while progress:
        progress = False
        for k, prog in S_.prog.items():
            while pc[k] < len(prog):
                waits, inc, idx = prog[pc[k]]
                if all(sem.get(n, 0) >= v for n, v in waits):
                    if inc:
                        sem[inc[0]] = sem.get(inc[0], 0) + inc[1]
                    pc[k] += 1
                    progress = True
                else:
                    break
    stuck = {k: (pc[k], len(p)) for k, p in S_.prog.items() if pc[k] < len(p)}
    for k in stuck:
        waits, inc, idx = S_.prog[k][pc[k]]
        print("STUCK", k, "at", pc[k], "of", len(S_.prog[k]), "ins#", idx, "waits", [(n, v, sem.get(n, 0)) for n, v in waits])
    return not stuck


class Carver:
    def __init__(self, t, lo, hi):
        self.t, self.lo, self.hi, self.p = t, lo, hi, lo

    def reset(self):
        self.p = self.lo

    def take(self, n, dt):
        nb = n * 2 if dt == F32 else n
        self.p = (self.p + 15) // 16 * 16
        a = self.t[:, self.p:self.p + nb]
        self.p += nb
        assert self.p <= self.hi, ("carver overflow", self.p, self.hi)
        return a.bitcast(F32) if dt == F32 else a


def build_nc(S, NSEQ):
    NT = S // 128
    NQC = S // 512
    NKB = S // 128
    nc = bass.Bass("TRN2", target_bir_lowering=False)

    def din(name, shape, dt=F32):
        return nc.dram_tensor(name, shape, dt, kind="ExternalInput").ap()

    x_d = din("x", [NSEQ, S, D])
    w_in_d = din("w_in", [D, INW])
    w_uq_d = din("w_uq", [256, 768])
    w_ukv_d = din("w_ukv", [128, 1024])
    w_out_d = din("w_out", [D, D])
    w_up_d = din("w_up", [D, DFF])
    w_dn_d = din("w_down", [DFF, D])
    rows_d = din("rows", [128, NR])
    cols_d = din("cols", [128, NCOL])
    lamb_d = din("lamb", [128, 256])
    cs_d = din("cs", [S, 128])
    strips_d = din("strips", [128, 4 * STRIP_W])
    ident_d = din("ident", [128, 128], BF16)
    out_d = nc.dram_tensor("out", [NSEQ, S, D], F32, kind="ExternalOutput").ap()
    wupb = nc.dram_tensor("wupb", [16, 128, 8 * 256], BF16, kind="Internal").ap()
    wdnb = nc.dram_tensor("wdnb", [DFF, D], BF16, kind="Internal").ap()

    REG_A = 8 * S
    REG_B = max(8 * S, 32768)
    WORK_N = 24320
    with ExitStack() as st:
        S_ = Sched(nc, st)
        sb = lambda name, shape, dt: st.enter_context(nc.sbuf_tensor("sb_" + name, shape, dt))
        REG = sb("reg", [128, REG_A + REG_B], BF16)
        WORK = sb("work", [128, WORK_N], BF16)
        WPB = sb("wpb", [128, 8 * 1536], BF16)
        cols = sb("cols", [128, NCOL], F32)
        lamt = sb("lamt", [128, 384], F32)
        lams = sb("lams", [128, 16], F32)
        ident = sb("ident", [128, 128], BF16)
        ones_b = sb("ones_b", [128, 128], BF16)
        ones_f = sb("ones_f", [128, 128], F32)
        wuq = sb("wuq", [128, 2, 768], BF16)
        wukv = sb("wukv", [128, 1024], BF16)
        ps = st.enter_context(nc.psum_tensor("ps", [128, 8, 512], F32))
        bps = [Buf("ps%d" % i) for i in range(8)]

        def psb(bank):
            return ps[:, bank, :].bitcast(BF16)

        b_const = Buf("const")
        b_lam = Buf("lam")
        b_wpb = Buf("wpb")
        b_wsmall = Buf("wsmall")
        b_scr = Buf("scratch_w")

        d_const = S_.new_dma_sem("const")
        d_wsm = S_.new_dma_sem("wsm")
        d_scr = S_.new_dma_sem("scr")
        d_wp = [S_.new_dma_sem("wp0"), S_.new_dma_sem("wp1")]
        d_x = [S_.new_dma_sem("x0"), S_.new_dma_sem("x1")]
        d_cs = [S_.new_dma_sem("cs%d" % i) for i in range(4)]
        d_out = [[S_.new_dma_sem("out%d_%d" % (k, j)) for j in range(4)] for k in range(2)]
        d_strip = S_.new_dma_sem("strip")
        d_wo = S_.new_dma_sem("wo")
        d_wu = [S_.new_dma_sem("wu%d" % i) for i in range(3)]
        d_wd = [S_.new_dma_sem("wd%d" % i) for i in range(3)]
        d_xc = [[S_.new_dma_sem("xc%d_%d" % (k, j)) for j in range(4)] for k in range(2)]

        QN = REG[:, 0:4 * S].rearrange("p (h s) -> p h s", h=4)
        KN = REG[:, 4 * S:8 * S].rearrange("p (h s) -> p h s", h=4)
        DQ = KN
        o2 = REG_A
        VT = REG[:, o2:o2 + 4 * S].rearrange("p (t n) -> p t n", n=512)
        DK = REG[:, o2:o2 + 4 * S].rearrange("p (h s) -> p h s", h=4)
        QR = REG[:, o2 + 4 * S:o2 + 6 * S].rearrange("p (h s) -> p h s", h=2)
        KR = REG[:, o2 + 6 * S:o2 + 8 * S].rearrange("p (h s) -> p h s", h=2)
        DVT = REG[:, o2 + 4 * S:o2 + 8 * S].rearrange("p (t n) -> p t n", n=512)
        bQ = [[Buf("q%d_%d" % (h, c)) for c in range(NQC)] for h in range(4)]
        bK = [[Buf("k%d_%d" % (h, c)) for c in range(NQC)] for h in range(4)]
        bV = Buf("v")
        bR3 = Buf("r3")

        def all_region_bufs():
            return [b for r in bQ for b in r] + [b for r in bK for b in r] + [bV, bR3]

        S_.dma("sp", ident[:], ident_d, d_const, writes=[b_const])
        S_.dma("sp", cols[:], cols_d, d_const, writes=[b_const])
        S_.dma("sp", lamt[:, 0:256], lamb_d, d_const, writes=[b_const])
        stg_q = WORK[:, 0:3072].bitcast(F32).rearrange("p (c n) -> p c n", c=2)
        stg_kv = WORK[:, 3072:5120].bitcast(F32)
        b_stg = Buf("stg")
        S_.dma("sp", stg_q, w_uq_d.rearrange("(c p) n -> p c n", p=128), d_wsm, writes=[b_stg])
        S_.dma("sp", stg_kv, w_ukv_d, d_wsm, writes=[b_stg])
        for c in range(2):
            S_.op("dve", lambda e: e.tensor_scalar(out=wuq[:, c, :], in0=stg_q[:, c, :], scalar1=cols[:, C_QA + c:C_QA + c + 1],
                                                   scalar2=None, op0=ALU.mult), reads=[b_stg, b_const], writes=[b_wsmall])
        S_.op("dve", lambda e: e.tensor_scalar(out=wukv[:], in0=stg_kv, scalar1=cols[:, C_KVA:C_KVA + 1], scalar2=None, op0=ALU.mult),
              reads=[b_stg, b_const], writes=[b_wsmall])
        S_.op("dve", lambda e: e.memset(ones_b[:], 1.0), writes=[b_lam])
        S_.op("dve", lambda e: e.memset(ones_f[:], 1.0), writes=[b_lam])
        lv = lamt[:, 0:256].rearrange("p (a b) -> p a b", a=2)
        lprod = lamt[:, 256:384].rearrange("p (a b) -> p a b", a=2)
        S_.op("dve", lambda e: e.tensor_tensor(out=lprod, in0=lv[:, :, 0:64], in1=lv[:, :, 64:128], op=ALU.mult),
              reads=[b_const], writes=[b_const])
        S_.op("dve", lambda e: e.reduce_sum(out=lams[:, 0:2], in_=lprod, axis=AX.X), reads=[b_const], writes=[b_lam])
        S_.op("act", lambda e: e.activation(out=lams[:, 2:4], in_=lams[:, 0:2], func=AF.Exp), reads=[b_lam], writes=[b_lam])
        S_.op("dve", lambda e: e.tensor_tensor(out=lams[:, 4:5], in0=lams[:, 2:3], in1=lams[:, 3:4], op=ALU.subtract),
              reads=[b_lam], writes=[b_lam])
        S_.op("dve", lambda e: e.tensor_scalar(out=lams[:, 5:6], in0=lams[:, 4:5], scalar1=LAM_INIT, scalar2=-1.0,
                                               op0=ALU.add, op1=ALU.mult), reads=[b_lam], writes=[b_lam])
        S_.op("dve", lambda e: e.tensor_scalar(out=lams[:, 6:7], in0=cols[:, C_DOUT:C_DOUT + 1], scalar1=1.0 - LAM_INIT,
                                               scalar2=None, op0=ALU.mult), reads=[b_const, b_lam], writes=[b_lam])
        neglam = lams[:, 5:6]
        doutc = lams[:, 6:7]
        for (dst, a, b) in ((7, C_MQN, C_MKN), (8, C_MQR, C_MKR), (9, C_DQW, C_DKW)):
            S_.op("dve", lambda e: e.tensor_tensor(out=lams[:, dst:dst + 1], in0=cols[:, a:a + 1], in1=cols[:, b:b + 1], op=ALU.mult),
                  reads=[b_const, b_lam], writes=[b_lam])
        wqk_n, wqk_r, wdqk = lams[:, 7:8], lams[:, 8:9], lams[:, 9:10]
        wup_src = w_up_d.rearrange("(c p) (g n) -> g p c n", p=128, n=256)
        wup_dst = wupb.rearrange("g p (c n) -> g p c n", n=256)
        for g in range(16):
            S_.dma("pool", wup_dst[g], wup_src[g], d_scr, writes=[b_scr])
        for g in range(8):
            S_.dma("pool", wdnb[g * 512:(g + 1) * 512, :], w_dn_d[g * 512:(g + 1) * 512, :], d_scr, writes=[b_scr])

        WK = Carver(WORK, 0, WORK_N)
        S_.barrier(skip=[d_scr])

        def fold_wp(mla, lo):
            ncol = 448 if mla else 1536
            WP = WPB[:, 0:8 * ncol].rearrange("p (c n) -> p c n", c=8)
            src = w_in_d[:, 0:448] if mla else w_in_d[:, 448:INW]
            stg = [WORK[:, lo + i * 3072:lo + (i + 1) * 3072].bitcast(F32) for i in range(2)]
            bstg = [Buf("wstg0"), Buf("wstg1")]
            for c in range(8):
                S_.dma("sp", stg[c % 2][:, 0:ncol], src[c * 128:(c + 1) * 128, :], d_wp[c % 2], writes=[bstg[c % 2]])
                S_.op("dve", lambda e: e.tensor_scalar(out=WP[:, c, :], in0=stg[c % 2][:, 0:ncol], scalar1=cols[:, C_ATTN + c:C_ATTN + c + 1],
                                                       scalar2=None, op0=ALU.mult), reads=[bstg[c % 2], b_const], writes=[b_wpb])

        def stage_A(s, mla, prefolded=False):
            ncol = 448 if mla else 1536
            WP = WPB[:, 0:8 * ncol].rearrange("p (c n) -> p c n", c=8)
            if not prefolded:
                fold_wp(mla, 0)
                S_.barrier(skip=[d_scr])
            WK.reset()
            junk = WK.take(1024, BF16); bjunk = Buf("junk")
            sets = []
            NS = 4
            xts = [WK.take(1024, F32) for _ in range(2)]
            bxs = [Buf("xt0"), Buf("xt1")]
            for k in range(NS):
                d = dict(k=k)
                d["cst"] = WK.take(128, F32) if mla else None; d["bcs"] = Buf("cs%d" % k)
                d["A"] = WK.take(1024, BF16); d["bA"] = Buf("A%d" % k)
                d["B"] = WK.take(1024, BF16); d["bB"] = Buf("B%d" % k)
                d["stt"] = WK.take(64, F32); d["bst"] = Buf("st%d" % k)
                d["sq"] = WK.take(768 if mla else 1024, F32); d["bsq"] = Buf("sq%d" % k)
                d["tmp2"] = WK.take(256, F32) if mla else None; d["btmp2"] = Buf("tmp2%d" % k)
                d["krot"] = WK.take(64, F32) if mla else None; d["bkrot"] = Buf("krot%d" % k)
                sets.append(d)

            def tile_gen(t, d):
                k = d["k"]
                cst, A, B, stt, sq, tmp2, krot = (d[n] for n in ("cst", "A", "B", "stt", "sq", "tmp2", "krot"))
                bcs, bA, bB, bst, bsq, btmp2, bkrot = (d[n] for n in ("bcs", "bA", "bB", "bst", "bsq", "btmp2", "bkrot"))
                xt, bx = xts[t % 2], bxs[t % 2]
                if mla:
                    pT = pP = pY = 2 * k
                    pX = 2 * k + 1
                else:
                    pT = pP = pY = 2 * k
                    pX = 2 * k + 1
                tok = slice(t * 128, (t + 1) * 128)
                hT3 = B.rearrange("p (c t) -> p c t", c=8)

                def rstd_from_ms(dst, src_):
                    S_.op("act", lambda e: e.activation(out=dst, in_=src_, func=AF.Ln, bias=EPS), reads=[bst], writes=[bst])
                    S_.op("act", lambda e: e.activation(out=dst, in_=dst, func=AF.Exp, scale=-0.5), reads=[bst], writes=[bst])

                def rope(dst, src_, nh, rd, wr):
                    cos = cst[:, 0:64].unsqueeze(1).broadcast_to([128, nh, 64])
                    s_lo = cst[:, 64:96].unsqueeze(1).broadcast_to([128, nh, 32])
                    s_hi = cst[:, 96:128].unsqueeze(1).broadcast_to([128, nh, 32])
                    t2 = tmp2[:, 0:nh * 64].rearrange("p (h d) -> p h d", h=nh)
                    S_.op("dve", lambda e: e.tensor_tensor(out=t2[:, :, 0:32], in0=src_[:, :, 32:64], in1=s_lo, op=ALU.mult),
                          reads=rd + [bcs], writes=[btmp2])
                    S_.op("dve", lambda e: e.tensor_tensor(out=t2[:, :, 32:64], in0=src_[:, :, 0:32], in1=s_hi, op=ALU.mult),
                          reads=rd + [bcs], writes=[btmp2])
                    S_.op("dve", lambda e: e.tensor_tensor(out=dst, in0=src_, in1=cos, op=ALU.mult),
                          reads=rd + [bcs], writes=wr)
                    S_.op("dve", lambda e: e.tensor_tensor(out=dst, in0=dst, in1=t2, op=ALU.add),
                          reads=wr + [btmp2], writes=wr)

                S_.dma("sp", xt, x_d[s, tok, :], d_x[t % 2], writes=[bx])
                if mla:
                    S_.dma("sp", cst, cs_d[tok, :], d_cs[k], writes=[bcs])
                yield
                S_.op("act", lambda e: e.activation(out=junk, in_=xt, func=AF.Square, scale=1.0 / 32.0,
                                                    accum_out=stt[:, 0:1]), reads=[bx], writes=[bjunk, bst])
                rstd_from_ms(stt[:, 1:2], stt[:, 0:1])
                yield
                if F_HB:
                    S_.op("act", lambda e: e.activation(out=A, in_=xt, func=AF.Copy, scale=stt[:, 1:2]),
                          reads=[bx, bst], writes=[bA])
                else:
                    S_.op("dve", lambda e: e.tensor_scalar(out=A, in0=xt, scalar1=stt[:, 1:2], scalar2=None, op0=ALU.mult),
                          reads=[bx, bst], writes=[bA])
                yield
                for c in range(8):
                    S_.op("pe", lambda e: e.transpose(out=psb(pT)[:, c * 128:(c + 1) * 128], in_=A[:, c * 128:(c + 1) * 128],
                                                      identity=ident[:]), reads=[bA, b_const], writes=[bps[pT]], inc=(c == 7))
                yield
                if A_ACTCOPY:
                    S_.op("act", lambda e: e.activation(out=B, in_=psb(pT), func=AF.Copy), reads=[bps[pT]], writes=[bB])
                else:
                    S_.op("dve", lambda e: e.tensor_copy(out=B, in_=psb(pT)), reads=[bps[pT]], writes=[bB])
                yield
                if mla:
                    for c in range(8):
                        S_.op("pe", lambda e: e.matmul(ps[:, pP, 0:448], lhsT=hT3[:, c, :], rhs=WP[:, c, 0:448],
                                                       start=(c == 0), stop=(c == 7)),
                              reads=[bB, b_wpb], writes=[bps[pP]], inc=(c == 7))
                else:
                    for nb, bank in ((2, pY), (0, pX)):
                        for c in range(8):
                            S_.op("pe", lambda e: e.matmul(ps[:, bank, :], lhsT=hT3[:, c, :], rhs=WP[:, c, nb * 512:(nb + 1) * 512],
                                                           start=(c == 0), stop=(c == 7)),
                                  reads=[bB, b_wpb], writes=[bps[bank]], inc=(c == 7))
                yield
                if mla:
                    cn = A
                    S_.op("act", lambda e: e.activation(out=junk[:, 0:256], in_=ps[:, pP, 0:256], func=AF.Square, scale=1.0 / 16.0,
                                                        accum_out=stt[:, 2:3]), reads=[bps[pP]], writes=[bjunk, bst])
                    S_.op("act", lambda e: e.activation(out=junk[:, 0:128], in_=ps[:, pP, 256:384], func=AF.Square,
                                                        scale=128.0 ** -0.5, accum_out=stt[:, 3:4]), reads=[bps[pP]], writes=[bjunk, bst])
                    rstd_from_ms(stt[:, 4:6], stt[:, 2:4])
                    S_.op("act", lambda e: e.activation(out=junk[:, 0:64], in_=ps[:, pP, 384:448], func=AF.Square, scale=192.0 ** -0.5,
                                                        accum_out=stt[:, 24:25]), reads=[bps[pP]], writes=[bjunk, bst])
                    yield
                    if F_CN:
                        S_.op("act", lambda e: e.activation(out=cn[:, 0:256], in_=ps[:, pP, 0:256], func=AF.Copy, scale=stt[:, 4:5]),
                              reads=[bps[pP], bst], writes=[bA])
                        S_.op("act", lambda e: e.activation(out=cn[:, 256:384], in_=ps[:, pP, 256:384], func=AF.Copy, scale=stt[:, 5:6]),
                              reads=[bps[pP], bst], writes=[bA])
                    else:
                        S_.op("dve", lambda e: e.tensor_scalar(out=cn[:, 0:256], in0=ps[:, pP, 0:256], scalar1=stt[:, 4:5], scalar2=None, op0=ALU.mult),
                              reads=[bps[pP], bst], writes=[bA])
                        S_.op("dve", lambda e: e.tensor_scalar(out=cn[:, 256:384], in0=ps[:, pP, 256:384], scalar1=stt[:, 5:6], scalar2=None, op0=ALU.mult),
                              reads=[bps[pP], bst], writes=[bA])
                    kr3 = krot.rearrange("p (h d) -> p h d", h=1)
                    rope(kr3, ps[:, pP, 384:448].rearrange("p (h d) -> p h d", h=1), 1, [bps[pP]], [bkrot])
                    yield
                    for c in range(3):
                        S_.op("pe", lambda e: e.transpose(out=psb(pT)[:, c * 128:(c + 1) * 128], in_=cn[:, c * 128:(c + 1) * 128],
                                                          identity=ident[:]), reads=[bA, b_const], writes=[bps[pT]], inc=(c == 2))
                    yield
                    S_.op("dve", lambda e: e.tensor_copy(out=B[:, 0:384], in_=psb(pT)[:, 0:384]), reads=[bps[pT]], writes=[bB])
                    cT3 = B[:, 0:384].rearrange("p (c t) -> p c t", c=3)
                    yield
                    for (bank, c0, w) in ((pX, 0, 512), (pY, 512, 256)):
                        for c in range(2):
                            S_.op("pe", lambda e: e.matmul(ps[:, bank, 0:w], lhsT=cT3[:, c, :], rhs=wuq[:, c, c0:c0 + w],
                                                           start=(c == 0), stop=(c == 1)),
                                  reads=[bB, b_wsmall], writes=[bps[bank]], inc=(c == 1))
                    yield
                    S_.op("act", lambda e: e.activation(out=sq[:, 0:512], in_=ps[:, pX, :], func=AF.Square, scale=192.0 ** -0.5),
                          reads=[bps[pX]], writes=[bsq])
                    S_.op("act", lambda e: e.activation(out=sq[:, 512:768], in_=ps[:, pY, 0:256], func=AF.Square, scale=192.0 ** -0.5),
                          reads=[bps[pY]], writes=[bsq])
                    yield
                    S_.op("dve", lambda e: e.reduce_sum(out=stt[:, 8:12], in_=sq[:, 0:512].rearrange("p (h d) -> p h d", h=4), axis=AX.X),
                          reads=[bsq], writes=[bst])
                    S_.op("dve", lambda e: e.reduce_sum(out=stt[:, 12:16], in_=sq[:, 512:768].rearrange("p (h d) -> p h d", h=4), axis=AX.X),
                          reads=[bsq], writes=[bst])
                    S_.op("dve", lambda e: e.tensor_tensor(out=stt[:, 16:20], in0=stt[:, 8:12], in1=stt[:, 12:16], op=ALU.add),
                          reads=[bst], writes=[bst])
                    yield
                    rstd_from_ms(stt[:, 20:24], stt[:, 16:20])
                    yield
                    rq_n = stt[:, 20:24].unsqueeze(2).broadcast_to([128, 4, 128])
                    rq_r = stt[:, 20:24].unsqueeze(2).broadcast_to([128, 4, 64])
                    Qtok = A
                    S_.op("dve", lambda e: e.tensor_tensor(out=Qtok[:, 0:512].rearrange("p (h d) -> p h d", h=4),
                                                           in0=ps[:, pX, :].rearrange("p (h d) -> p h d", h=4), in1=rq_n, op=ALU.mult),
                          reads=[bps[pX], bst], writes=[bA])
                    qr3 = sq[:, 0:256].rearrange("p (h d) -> p h d", h=4)
                    rope(qr3, ps[:, pY, 0:256].rearrange("p (h d) -> p h d", h=4), 4, [bps[pY]], [bsq])
                    S_.op("dve", lambda e: e.tensor_tensor(out=Qtok[:, 512:768].rearrange("p (h d) -> p h d", h=4), in0=qr3, in1=rq_r,
                                                           op=ALU.mult), reads=[bsq, bst], writes=[bA])
                    yield
                    for c in range(6):
                        S_.op("pe", lambda e: e.transpose(out=psb(pT)[:, c * 128:(c + 1) * 128], in_=Qtok[:, c * 128:(c + 1) * 128],
                                                          identity=ident[:]), reads=[bA, b_const], writes=[bps[pT]], inc=(c == 5))
                    yield
                    qcb = [bQ[h][t // 4] for h in range(4)]
                    if A_ACTCOPY:
                        S_.op("act", lambda e: e.activation(out=QN[:, :, tok], in_=psb(pT)[:, 0:512].rearrange("p (h t) -> p h t", h=4),
                                                            func=AF.Copy), reads=[bps[pT]], writes=qcb)
                        S_.op("act", lambda e: e.activation(out=QR[:, :, tok], in_=psb(pT)[:, 512:768].rearrange("p (h t) -> p h t", h=2),
                                                            func=AF.Copy), reads=[bps[pT]], writes=[bR3])
                    else:
                        S_.op("dve", lambda e: e.tensor_copy(out=QN[:, :, tok], in_=psb(pT)[:, 0:512].rearrange("p (h t) -> p h t", h=4)),
                              reads=[bps[pT]], writes=qcb)
                        S_.op("dve", lambda e: e.tensor_copy(out=QR[:, :, tok], in_=psb(pT)[:, 512:768].rearrange("p (h t) -> p h t", h=2)),
                              reads=[bps[pT]], writes=[bR3])
                    yield
                    for (bank, c0) in ((pX, 0), (pY, 512)):
                        S_.op("pe", lambda e: e.matmul(ps[:, bank, :], lhsT=cT3[:, 2, :], rhs=wukv[:, c0:c0 + 512], start=True, stop=True),
                              reads=[bB, b_wsmall], writes=[bps[bank]])
                    yield
                    S_.op("act", lambda e: e.activation(out=sq[:, 0:512], in_=ps[:, pX, :], func=AF.Square, scale=192.0 ** -0.5),
                          reads=[bps[pX]], writes=[bsq])
                    S_.op("act", lambda e: e.activation(out=VT[:, t, :], in_=ps[:, pY, :], func=AF.Copy), reads=[bps[pY]], writes=[bV])
                    yield
                    S_.op("dve", lambda e: e.reduce_sum(out=stt[:, 8:12], in_=sq[:, 0:512].rearrange("p (h d) -> p h d", h=4), axis=AX.X),
                          reads=[bsq], writes=[bst])
                    S_.op("dve", lambda e: e.tensor_scalar(out=stt[:, 16:20], in0=stt[:, 8:12], scalar1=stt[:, 24:25], scalar2=None,
                                                           op0=ALU.add), reads=[bst], writes=[bst])
                    yield
                    rstd_from_ms(stt[:, 28:32], stt[:, 16:20])
                    yield
                    rk_n = stt[:, 28:32].unsqueeze(2).broadcast_to([128, 4, 128])
                    rk_r = stt[:, 28:32].unsqueeze(2).broadcast_to([128, 4, 64])
                    Ktok = A
                    S_.op("dve", lambda e: e.tensor_tensor(out=Ktok[:, 0:512].rearrange("p (h d) -> p h d", h=4),
                                                           in0=ps[:, pX, :].rearrange("p (h d) -> p h d", h=4), in1=rk_n, op=ALU.mult),
                          reads=[bps[pX], bst], writes=[bA])
                    S_.op("dve", lambda e: e.tensor_tensor(out=Ktok[:, 512:768].rearrange("p (h d) -> p h d", h=4),
                                                           in0=krot.unsqueeze(1).broadcast_to([128, 4, 64]), in1=rk_r, op=ALU.mult),
                          reads=[bkrot, bst], writes=[bA])
                    yield
                    for c in range(6):
                        S_.op("pe", lambda e: e.transpose(out=psb(pT)[:, c * 128:(c + 1) * 128], in_=Ktok[:, c * 128:(c + 1) * 128],
                                                          identity=ident[:]), reads=[bA, b_const], writes=[bps[pT]], inc=(c == 5))
                    yield
                    kcb = [bK[h][t // 4] for h in range(4)]
                    S_.op("dve", lambda e: e.tensor_scalar(out=KN[:, :, tok], in0=psb(pT)[:, 0:512].rearrange("p (h t) -> p h t", h=4),
                                                           scalar1=wqk_n, scalar2=None, op0=ALU.mult),
                          reads=[bps[pT], b_lam], writes=kcb)
                    S_.op("dve", lambda e: e.tensor_scalar(out=KR[:, :, tok], in0=psb(pT)[:, 512:768].rearrange("p (h t) -> p h t", h=2),
                                                           scalar1=wqk_r, scalar2=None, op0=ALU.mult),
                          reads=[bps[pT], b_lam], writes=[bR3])
                    yield
                else:
                    cn = A
                    S_.op("act", lambda e: e.activation(out=DVT[:, t, :], in_=ps[:, pY, :], func=AF.Copy), reads=[bps[pY]], writes=[bR3])
                    for (which, s0, r0, dst) in ((0, 8, 32, 0), (1, 16, 40, 512)):
                        if which == 1:
                            for c in range(8):
                                S_.op("pe", lambda e: e.matmul(ps[:, pX, :], lhsT=hT3[:, c, :], rhs=WP[:, c, 512:1024],
                                                               start=(c == 0), stop=(c == 7)),
                                      reads=[bB, b_wpb], writes=[bps[pX]], inc=(c == 7))
                            yield
                        sqh = sq[:, dst:dst + 512]
                        S_.op("act", lambda e: e.activation(out=sqh, in_=ps[:, pX, :], func=AF.Square, scale=0.125),
                              reads=[bps[pX]], writes=[bsq])
                        yield
                        S_.op("dve", lambda e: e.reduce_sum(out=stt[:, s0:s0 + 8], in_=sqh.rearrange("p (h d) -> p h d", h=8), axis=AX.X),
                              reads=[bsq], writes=[bst])
                        yield
                        rstd_from_ms(stt[:, r0:r0 + 8], stt[:, s0:s0 + 8])
                        yield
                        S_.op("dve", lambda e: e.tensor_tensor(out=cn[:, dst:dst + 512].rearrange("p (h d) -> p h d", h=8),
                                                               in0=ps[:, pX, :].rearrange("p (h d) -> p h d", h=8),
                                                               in1=stt[:, r0:r0 + 8].unsqueeze(2).broadcast_to([128, 8, 64]), op=ALU.mult),
                              reads=[bps[pX], bst], writes=[bA])
                        yield
                    for c in range(8):
                        S_.op("pe", lambda e: e.transpose(out=psb(pT)[:, c * 128:(c + 1) * 128], in_=cn[:, c * 128:(c + 1) * 128],
                                                          identity=ident[:]), reads=[bA, b_const], writes=[bps[pT]], inc=(c == 7))
                    yield
                    kcb = [bK[h][t // 4] for h in range(4)]
                    S_.op("dve", lambda e: e.tensor_copy(out=DQ[:, :, tok], in_=psb(pT)[:, 0:512].rearrange("p (h t) -> p h t", h=4)),
                          reads=[bps[pT]], writes=kcb)
                    if A_ACTCOPY:
                        S_.op("act", lambda e: e.activation(out=DK[:, :, tok], in_=psb(pT)[:, 512:1024].rearrange("p (h t) -> p h t", h=4),
                                                            func=AF.Copy), reads=[bps[pT]], writes=[bV])
                    else:
                        S_.op("dve", lambda e: e.tensor_scalar(out=DK[:, :, tok], in0=psb(pT)[:, 512:1024].rearrange("p (h t) -> p h t", h=4),
                                                               scalar1=wdqk, scalar2=None, op0=ALU.mult),
                              reads=[bps[pT], b_lam], writes=[bV])
                    yield

            active = []
            nxt = 0
            STAG = 4 if mla else 3
            while nxt < NT or active:
                if nxt < NT and (not active or (A_INTERLEAVE and len(active) < NS and active[-1][1] >= STAG)):
                    d = sets[nxt % NS]
                    active.append([tile_gen(nxt, d), 0])
                    nxt += 1
                for a in list(active):
                    try:
                        next(a[0])
                        a[1] += 1
                    except StopIteration:
                        active.remove(a)

        def stage_B(mla):
            WK.reset()
            PT = [WK.take(1024, BF16) for _ in range(3)]
            bpt = [Buf("pt%d" % i) for i in range(3)]
            QM = [WK.take(512, BF16) for _ in range(2)]
            bqm = [Buf("qm0"), Buf("qm1")]
            tr = WK.take(512, F32); btr = Buf("tr")
            to = WK.take(512, F32); bto = Buf("to")
            tt = WK.take(512, F32); btt = Buf("tt")
            tsq, btsq, trs, btrs = tr, btr, tt, btt
            bstf = Buf("stf")
            if not mla:
                stf = WK.take(4 * STRIP_W, F32)
                BH = WPB[:, 0:4 * STRIP_W].rearrange("p (h w) -> p h w", h=4)
                BL = WPB[:, 4 * STRIP_W:8 * STRIP_W].rearrange("p (h w) -> p h w", h=4)
                S_.dma("sp", stf, strips_d, d_strip, writes=[bstf])
                S_.op("dve", lambda e: e.tensor_scalar(out=stf, in0=stf, scalar1=8.0, scalar2=None, op0=ALU.mult),
                      reads=[bstf], writes=[bstf])
                S_.op("dve", lambda e: e.tensor_copy(out=WPB[:, 0:4 * STRIP_W], in_=stf), reads=[bstf], writes=[b_wpb])
                S_.op("dve", lambda e: e.tensor_tensor(out=WPB[:, 4 * STRIP_W:8 * STRIP_W], in0=stf, in1=WPB[:, 0:4 * STRIP_W],
                                                       op=ALU.subtract), reads=[bstf, b_wpb], writes=[b_wpb])
            bacc = [Buf("acc0"), Buf("acc1")]
            units = []
            if mla:
                for h in range(4):
                    for qc in range(NQC):
                        units.append(dict(h=h, qc=qc, m=0))
            else:
                for h in range(4):
                    for qc in range(NQC):
                        for m in range(2):
                            units.append(dict(h=h, qc=qc, m=m))
            NSL = NKB // 2
            slots = [(ui, j) for ui in range(len(units)) for j in range(NSL)]
            deferred = []
            L = 1

            def prep_qm(ui):
                u = units[ui]
                h, qc, m = u["h"], u["qc"], u["m"]
                qs = slice(qc * 512, (qc + 1) * 512)
                qm = QM[ui % 2]
                if mla:
                    p0 = (h % 2) * 64
                    srcq = QR[p0:p0 + 64, h // 2, qs]
                    rb_ = [bR3]
                else:
                    p0 = m * 64
                    srcq = DQ[p0:p0 + 64, h, qs]
                    rb_ = [bK[h][qc]]
                pd = 64 - p0
                S_.op("pool", lambda e: e.memset(qm[pd:pd + 64, :], 0.0), writes=[bqm[ui % 2]])
                S_.op("pool", lambda e: e.tensor_copy(out=qm[p0:p0 + 64, :], in_=srcq), reads=rb_, writes=[bqm[ui % 2]])

            def bias_kind(qc, kb):
                if mla:
                    return "none"
                if kb < 4 * qc - 1:
                    return "neg"
                if kb > 4 * qc + 4:
                    return "pos"
                return "near"

            def emit_scores(g):
                ui, j = slots[g]
                u = units[ui]
                h, qc, m = u["h"], u["qc"], u["m"]
                qs = slice(qc * 512, (qc + 1) * 512)
                r = g % 2
                if g == 0:
                    prep_qm(0)
                if j == NSL // 2 and ui + 1 < len(units):
                    prep_qm(ui + 1)
                qm = QM[ui % 2]
                for t_ in range(2):
                    kb = 2 * j + t_
                    near = bias_kind(qc, kb) == "near"
                    ks = slice(kb * 128, (kb + 1) * 128)
                    bank = 2 * r + t_
                    last = (t_ == 1)
                    if mla:
                        S_.op("pe", lambda e: e.matmul(ps[:, bank, :], lhsT=KN[:, h, ks], rhs=QN[:, h, qs], start=True, stop=False),
                              reads=[bK[h][kb // 4], bQ[h][qc]], writes=[bps[bank]], inc=False)
                        S_.op("pe", lambda e: e.matmul(ps[:, bank, :], lhsT=KR[:, h // 2, ks], rhs=qm, start=False, stop=True),
                              reads=[bR3, bqm[ui % 2]], writes=[bps[bank]], inc=last)
                    else:
                        S_.op("pe", lambda e: e.matmul(ps[:, bank, :], lhsT=DK[:, h, ks], rhs=qm, start=True, stop=(not near)),
                              reads=[bV, bqm[ui % 2]], writes=[bps[bank]], inc=(not near))
                        if near:
                            off = DMAX - (kb - 4 * qc) * 128
                            S_.op("pe", lambda e: e.matmul(ps[:, bank, :], lhsT=ident[:], rhs=BH[:, h, off:off + 512], start=False, stop=False),
                                  reads=[b_wpb, b_const], writes=[bps[bank]], inc=False)
                            S_.op("pe", lambda e: e.matmul(ps[:, bank, :], lhsT=ident[:], rhs=BL[:, h, off:off + 512], start=False, stop=True),
                                  reads=[b_wpb, b_const], writes=[bps[bank]], inc=True)

            def emit_exp_pv(g):
                ui, j = slots[g]
                u = units[ui]
                h, qc, m = u["h"], u["qc"], u["m"]
                r = g % 2
                pt = PT[g % 3]
                bp = bpt[g % 3]
                ob, sbk = (4, 5) if ui % 2 == 0 else (6, 7)
                if mla:
                    scale = 192.0 ** -0.5
                    bias = 0.0
                    vb = bV
                else:
                    scale = 0.125
                    vb = bR3

                def bias_of(kind):
                    if kind == "neg":
                        return cols[:, C_NEG + h:C_NEG + h + 1]
                    if kind == "pos":
                        return cols[:, C_POS + h:C_POS + h + 1]
                    return 0.0
                k0, k1 = bias_kind(qc, 2 * j), bias_kind(qc, 2 * j + 1)
                if k0 == k1:
                    S_.op("act", lambda e: e.activation(out=pt.rearrange("p (b n) -> p b n", b=2), in_=ps[:, 2 * r:2 * r + 2, :],
                                                        func=AF.Exp, scale=scale, bias=bias_of(k0)),
                          reads=[bps[2 * r], bps[2 * r + 1], b_const], writes=[bp])
                else:
                    for t_, kk in ((0, k0), (1, k1)):
                        S_.op("act", lambda e: e.activation(out=pt[:, t_ * 512:(t_ + 1) * 512], in_=ps[:, 2 * r + t_, :],
                                                            func=AF.Exp, scale=scale, bias=bias_of(kk)),
                              reads=[bps[2 * r + t_], b_const], writes=[bp])
                for t_ in range(2):
                    kb = 2 * j + t_
                    vap = (VT if mla else DVT)[:, kb, h * 128:(h + 1) * 128]
                    S_.op("pe", lambda e: e.matmul(ps[:, ob, :], lhsT=vap, rhs=pt[:, t_ * 512:(t_ + 1) * 512],
                                                   start=(kb == 0), stop=(kb == NKB - 1)),
                          reads=[vb, bp], writes=[bacc[ui % 2]], inc=False)
                for t_ in range(2):
                    kb = 2 * j + t_
                    S_.op("pe", lambda e: e.matmul(ps[:, sbk, :], lhsT=ones_b[:], rhs=pt[:, t_ * 512:(t_ + 1) * 512],
                                                   start=(kb == 0), stop=(kb == NKB - 1)),
                          reads=[b_lam, bp], writes=[bacc[ui % 2]], inc=(t_ == 1))
                if j == NSL - 1:
                    finalize(ui, g)

            def finalize(ui, g):
                u = units[ui]
                h, qc, m = u["h"], u["qc"], u["m"]
                ob, sbk = (4, 5) if ui % 2 == 0 else (6, 7)
                ba = bacc[ui % 2]
                qs = slice(qc * 512, (qc + 1) * 512)
                S_.op("dve", lambda e: e.reciprocal(out=tr, in_=ps[:, sbk, :]), reads=[ba], writes=[btr])
                if mla:
                    S_.op("dve", lambda e: e.tensor_tensor(out=QN[:, h, qs], in0=ps[:, ob, :], in1=tr, op=ALU.mult),
                          reads=[ba, btr], writes=[bQ[h][qc]])
                    return
                if m == 0:
                    S_.op("dve", lambda e: e.tensor_tensor(out=to, in0=ps[:, ob, :], in1=tr, op=ALU.mult),
                          reads=[ba, btr], writes=[bto])
                    return
                S_.op("dve", lambda e: e.tensor_tensor(out=tt, in0=ps[:, ob, :], in1=tr, op=ALU.mult),
                      reads=[ba, btr], writes=[btt])
                S_.op("dve", lambda e: e.scalar_tensor_tensor(out=to, in0=tt, scalar=neglam, in1=to, op0=ALU.mult, op1=ALU.add),
                      reads=[btt, bto, b_lam], writes=[bto])
                S_.op("dve", lambda e: e.tensor_tensor(out=tsq, in0=to, in1=to, op=ALU.mult), reads=[bto], writes=[btsq])

                def t1():
                    S_.op("pe", lambda e: e.matmul(ps[:, sbk, :], lhsT=ones_f[:], rhs=tsq, start=True, stop=True),
                          reads=[btsq, b_lam], writes=[ba])

                def t2():
                    S_.op("act", lambda e: e.activation(out=trs, in_=ps[:, sbk, :], func=AF.Ln, scale=1.0 / 128.0, bias=EPS),
                          reads=[ba], writes=[btrs])

                def t3():
                    S_.op("act", lambda e: e.activation(out=trs, in_=trs, func=AF.Exp, scale=-0.5), reads=[btrs], writes=[btrs])
                    S_.op("dve", lambda e: e.scalar_tensor_tensor(out=DQ[:, h, qs], in0=to, scalar=doutc, in1=trs,
                                                                  op0=ALU.mult, op1=ALU.mult),
                          reads=[bto, btrs, b_lam], writes=[bK[h][qc]])
                o3 = min(8, NSL - 1)
                o2 = max(1, o3 - 2)
                o1 = max(1, o3 - 3)
                deferred.append((g + o1, t1))
                deferred.append((g + o2, t2))
                deferred.append((g + o3, t3))

            NG = len(slots)
            for g in range(NG + L):
                if g < NG:
                    emit_scores(g)
                if g >= L:
                    emit_exp_pv(g - L)
                while deferred and deferred[0][0] <= g:
                    deferred.pop(0)[1]()
            while deferred:
                deferred.pop(0)[1]()

        def stage_C(s):
            WK.reset()
            x1s = [[WK.take(1024, F32) for _ in range(4)], None]
            bx1s = [[Buf("x1_%d_%d" % (k, j)) for j in range(4)] for k in range(2)]
            hb2 = [WK.take(1024, BF16) for _ in range(2)]
            bhb2 = [Buf("hb2_0"), Buf("hb2_1")]
            h2T = WK.take(8 * 512, BF16); bh2T = Buf("h2T")
            h2T3 = h2T.rearrange("p (c t) -> p c t", c=8)
            junk = WK.take(1024, BF16); bjunk = Buf("junkc")
            rows = WK.take(1024, F32); b_rows = Buf("rows")
            S_.dma("sp", rows, rows_d, d_strip, writes=[b_rows])
            stt = WK.take(16, F32); bst = [Buf("stc%d" % j) for j in range(4)]
            rl = [WK.take(512, F32) for _ in range(2)]
            brl = [Buf("rl0"), Buf("rl1")]
            o2_ = REG_A
            wout = REG[:, o2_:o2_ + 8192].rearrange("p (c n) -> p c n", c=8); bwout = Buf("wout")
            aT = REG[:, o2_ + 8192:o2_ + 8192 + 16384].rearrange("p (f t) -> p f t", f=32); baT = Buf("aT")
            x1s[1] = [REG[:, o2_ + 24576 + j * 2048:o2_ + 24576 + (j + 1) * 2048].bitcast(F32) for j in range(4)]
            wu = [WPB[:, i * 2048:(i + 1) * 2048].rearrange("p (c n) -> p c n", c=8) for i in range(3)]
            wd = [WPB[:, 6144 + i * 2048:6144 + (i + 1) * 2048].rearrange("p (f n) -> p f n", f=2) for i in range(3)]
            bwu = [Buf("wu%d" % i) for i in range(3)]
            bwd = [Buf("wd%d" % i) for i in range(3)]
            S_.dma("pool", wout, w_out_d.rearrange("(c p) n -> p c n", p=128), d_wo, writes=[bwout])
            wupv = wupb.rearrange("g p (c n) -> g p c n", n=256)
            wdnv = wdnb.rearrange("(g f p) n -> g p f n", f=2, p=128)

            def load_x(ck):
                k = ck % 2
                for j in range(4):
                    tok = slice(ck * 512 + j * 128, ck * 512 + (j + 1) * 128)
                    S_.dma("sp", x1s[k][j], x_d[s, tok, :], d_xc[k][j], writes=[bx1s[k][j]])

            load_x(0)
            gi = 0
            for ck in range(NQC):
                k = ck % 2
                x1, bx1 = x1s[k], bx1s[k]
                rdb = [bQ[h][ck] for h in range(4)] + [bK[h][ck] for h in range(4)] + [bwout]
                for j in range(4):
                    tq = slice(ck * 512 + j * 128, ck * 512 + (j + 1) * 128)
                    for half in range(2):
                        bank = 2 * j + half
                        for c in range(8):
                            lhs = QN[:, c, tq] if c < 4 else DQ[:, c - 4, tq]
                            S_.op("pe", lambda e: e.matmul(ps[:, bank, :], lhsT=lhs, rhs=wout[:, c, half * 512:(half + 1) * 512],
                                                           start=(c == 0), stop=(c == 7)),
                                  reads=rdb, writes=[bps[bank]], inc=(c == 7))
                if ck + 1 < NQC:
                    load_x(ck + 1)
                def p2(j):
                    S_.op("dve", lambda e: e.tensor_tensor(out=x1[j], in0=x1[j], in1=ps[:, 2 * j:2 * j + 2, :].rearrange("p b n -> p (b n)"),
                                                           op=ALU.add), reads=[bx1[j], bps[2 * j], bps[2 * j + 1]], writes=[bx1[j]])
                    S_.op("act", lambda e: e.activation(out=junk, in_=x1[j], func=AF.Square, scale=1.0 / 32.0,
                                                        accum_out=stt[:, 2 * j:2 * j + 1]), reads=[bx1[j]], writes=[bjunk, bst[j]])
                    S_.op("act", lambda e: e.activation(out=stt[:, 2 * j + 1:2 * j + 2], in_=stt[:, 2 * j:2 * j + 1], func=AF.Ln, bias=EPS),
                          reads=[bst[j]], writes=[bst[j]])
                    S_.op("act", lambda e: e.activation(out=stt[:, 2 * j + 1:2 * j + 2], in_=stt[:, 2 * j + 1:2 * j + 2], func=AF.Exp, scale=-0.5),
                          reads=[bst[j]], writes=[bst[j]])
                    S_.op("dve", lambda e: e.scalar_tensor_tensor(out=hb2[j % 2], in0=x1[j], scalar=stt[:, 2 * j + 1:2 * j + 2],
                                                                  in1=rows[:, R_MLP:R_MLP + 1024], op0=ALU.mult, op1=ALU.mult),
                          reads=[bx1[j], bst[j], b_rows], writes=[bhb2[j % 2]])

                def p3(j):
                    bank = 2 * j
                    for c in range(8):
                        S_.op("pe", lambda e: e.transpose(out=psb(bank)[:, c * 128:(c + 1) * 128], in_=hb2[j % 2][:, c * 128:(c + 1) * 128],
                                                          identity=ident[:]), reads=[bhb2[j % 2], b_const], writes=[bps[bank]], inc=(c == 7))
                    S_.op("dve", lambda e: e.tensor_copy(out=h2T3[:, :, j * 128:(j + 1) * 128],
                                                         in_=psb(bank).rearrange("p (c t) -> p c t", c=8)),
                          reads=[bps[bank]], writes=[bh2T])
                p2(0); p2(1); p3(0); p2(2); p3(1); p2(3); p3(2); p3(3)
                for g in range(16):
                    r = (gi + g) % 3
                    S_.dma("pool", wu[r], wupv[g], d_wu[r], reads=[b_scr], writes=[bwu[r]])
                    for fl in range(2):
                        f = g * 2 + fl
                        bank = 1 + 2 * (f % 2)
                        for c in range(8):
                            S_.op("pe", lambda e: e.matmul(ps[:, bank, :], lhsT=wu[r][:, c, fl * 128:(fl + 1) * 128], rhs=h2T3[:, c, :],
                                                           start=(c == 0), stop=(c == 7)),
                                  reads=[bwu[r], bh2T], writes=[bps[bank]], inc=(c == 7))
                        S_.op("act", lambda e: e.activation(out=rl[f % 2], in_=ps[:, bank, :], func=AF.Relu),
                              reads=[bps[bank]], writes=[brl[f % 2]])
                        S_.op("dve", lambda e: e.tensor_tensor(out=aT[:, f, :], in0=rl[f % 2], in1=rl[f % 2], op=ALU.mult),
                              reads=[brl[f % 2]], writes=[baT])
                for g in range(16):
                    r = (gi + g) % 3
                    S_.dma("pool", wd[r], wdnv[g], d_wd[r], reads=[b_scr], writes=[bwd[r]])
                    for fl in range(2):
                        f = g * 2 + fl
                        for j in range(4):
                            for half in range(2):
                                bank = j * 2 + half
                                S_.op("pe", lambda e: e.matmul(ps[:, bank, :], lhsT=aT[:, f, j * 128:(j + 1) * 128],
                                                               rhs=wd[r][:, fl, half * 512:(half + 1) * 512],
                                                               start=(f == 0), stop=(f == 31)),
                                      reads=[baT, bwd[r]], writes=[bps[bank]], inc=(f == 31 or (j == 3 and half == 1)))
                gi += 16
                for j in range(4):
                    tok = slice(ck * 512 + j * 128, ck * 512 + (j + 1) * 128)
                    S_.op("dve", lambda e: e.tensor_tensor(out=x1[j], in0=x1[j],
                                                           in1=ps[:, 2 * j:2 * j + 2, :].rearrange("p b n -> p (b n)"), op=ALU.add),
                          reads=[bx1[j], bps[2 * j], bps[2 * j + 1]], writes=[bx1[j]])
                    S_.dma("sp", out_d[s, tok, :], x1[j], d_out[k][j], reads=[bx1[j]])

        for s in range(NSEQ):
            stage_A(s, True)
            S_.barrier()
            fold_wp(False, WORK_N - 6144)
            stage_B(True)
            S_.barrier()
            stage_A(s, False, prefolded=True)
            S_.barrier()
            stage_B(False)
            S_.barrier()
            stage_C(s)
            S_.barrier()
        S_.barrier(("sp",))
        print("build: n_ins", S_.n_ins, "n_wait", S_.n_wait, "nsem", S_.nsem)
        if DEBUG_SIM:
            print("simulate ok:", simulate(S_))
    return nc


def _t5_bucket_np(rel):
    half = 16
    ret = np.where(rel > 0, half, 0)
    n = np.abs(rel)
    max_exact = half // 2
    large = max_exact + (np.log(np.maximum(n, 1).astype(np.float32) / np.float32(max_exact))
                         / np.float32(math.log(128 / max_exact)) * np.float32(half - max_exact)).astype(np.int32)
    large = np.minimum(large, half - 1)
    return ret + np.where(n < max_exact, n, large)


def _host_tables(S):
    inv = (np.float32(10000.0) ** (-np.arange(0, 64, 2, dtype=np.float32) / np.float32(64))).astype(np.float32)
    ang = np.arange(S, dtype=np.float32)[:, None] * inv[None, :]
    ang = np.concatenate([ang, ang], axis=-1)
    cos = np.cos(ang).astype(np.float32)
    sin = np.sin(ang).astype(np.float32)
    sinS = sin.copy()
    sinS[:, :32] = -sinS[:, :32]
    cs = np.concatenate([cos, sinS], axis=1).astype(np.float32)
    k = np.arange(128)[:, None]
    j = np.arange(STRIP_W)[None, :]
    bucket = _t5_bucket_np((k - j + DMAX).astype(np.int32))
    return cs, bucket


def prepare_inputs(S, x_core_list, p):
    cs, bucket = _host_tables(S)
    f32 = np.float32
    w_uq = np.asarray(p["w_uq"][0], f32)
    perm_q = np.concatenate([np.arange(h * 192, h * 192 + 128) for h in range(4)]
                            + [np.arange(h * 192 + 128, h * 192 + 192) for h in range(4)])
    w_ukv = np.asarray(p["w_ukv"][0], f32)
    perm_kv = np.concatenate([np.arange(h * 256, h * 256 + 128) for h in range(4)]
                             + [np.arange(h * 256 + 128, h * 256 + 256) for h in range(4)])
    row = np.asarray(p["mlp_norm_w"][0], f32)
    assert row.shape[0] == NR
    rows = np.ascontiguousarray(np.broadcast_to(row[None, :], (128, NR)))
    rb = np.asarray(p["rel_bias"], f32)
    cols = np.zeros((128, NCOL), f32)
    cols[:, C_NEG:C_NEG + 4] = rb[15][None, :]
    cols[:, C_POS:C_POS + 4] = rb[31][None, :]
    cols[:, C_DOUT] = np.asarray(p["diff_out_norm_w"][0], f32)
    cols[:, C_ATTN:C_ATTN + 8] = np.asarray(p["attn_norm_w"][0], f32).reshape(8, 128).T
    cols[:, C_QA:C_QA + 2] = np.asarray(p["q_a_norm_w"][0], f32).reshape(2, 128).T
    cols[:, C_KVA] = np.asarray(p["kv_a_norm_w"][0], f32)
    mq = np.asarray(p["mla_q_norm_w"][0], f32); mk = np.asarray(p["mla_k_norm_w"][0], f32)
    cols[:, C_MQN] = mq[0:128]; cols[:, C_MQR] = np.tile(mq[128:192], 2)
    cols[:, C_MKN] = mk[0:128]; cols[:, C_MKR] = np.tile(mk[128:192], 2)
    cols[:, C_DQW] = np.tile(np.asarray(p["diff_q_norm_w"][0], f32), 2)
    cols[:, C_DKW] = np.tile(np.asarray(p["diff_k_norm_w"][0], f32), 2)
    lam = np.concatenate([np.asarray(p[k][0], f32) for k in ("lambda_q1", "lambda_k1", "lambda_q2", "lambda_k2")])
    lamb = np.ascontiguousarray(np.broadcast_to(lam[None, :], (128, 256)))
    strips = np.ascontiguousarray(np.transpose(rb[bucket], (0, 2, 1)).reshape(128, 4 * STRIP_W))
    shared = {
        "w_in": np.ascontiguousarray(p["w_in"][0], dtype=f32),
        "w_uq": np.ascontiguousarray(w_uq[:, perm_q]),
        "w_ukv": np.ascontiguousarray(w_ukv[:, perm_kv]),
        "w_out": np.ascontiguousarray(p["w_out"][0], dtype=f32),
        "w_up": np.ascontiguousarray(p["w_up"][0], dtype=f32),
        "w_down": np.ascontiguousarray(p["w_down"][0], dtype=f32),
        "rows": rows, "cols": cols, "lamb": lamb, "cs": cs, "strips": strips,
        "ident": np.eye(128).astype(ml_dtypes.bfloat16),
    }
    return [dict(shared, x=np.ascontiguousarray(xc, dtype=f32)) for xc in x_core_list]


def kernel(**inputs):
    x = np.asarray(inputs["x"], np.float32)
    B, S, _ = x.shape
    n = 8
    nseq = B // n
    p = {k: np.asarray(v) for k, v in inputs.items() if k != "x"}
    in_maps = prepare_inputs(S, [x[i * nseq:(i + 1) * nseq] for i in range(n)], p)
    nc = build_nc(S, nseq)
    res = run_bass_kernel_spmd(nc, in_maps, core_ids=list(range(n)))
    return np.concatenate([np.asarray(r["out"], np.float32) for r in res.results], axis=0)
```
